# Optimizing a Trainium2 kernel written in Bass

```python
import math
import jax
import jax.numpy as jnp
from jax import lax
import numpy as np

D_MODEL = 2048
BATCH = 8
SEQ = 2048
DEPTH = 2
DEC_BATCH = 8
DEC_SEQ = 4096
PAST_LEN = 128

PLE_DIM = 256
GRID_W = 64
A_HEADS = 8
A_QK_DIM = 64
A_V_DIM = 2 * A_QK_DIM
A_WIDTH = A_HEADS * A_V_DIM
Q_BLOCK = 128
T5_BUCKETS = 32
T5_MAX_DIST = 128
B_HEADS = 8
B_HEAD_DIM = 128
B_WIDTH = B_HEADS * B_HEAD_DIM
NA_ROWS = 8
NA_COLS = 16
AB_IN = 3 * A_WIDTH + 3 * B_WIDTH
AB_OUT = A_WIDTH + B_WIDTH
C_WIDTH = D_MODEL
SHORT_CONV = 3
FILTER_EMB = 33
FILTER_HIDDEN = 64
FILTER_TARGET = 1e-2
FAST_DECAY_PCT = 0.3
SLOW_DECAY_PCT = 1.5
D_FF = ((8 * D_MODEL + 3 * 256 - 1) // (3 * 256)) * 256
N_EVEN = (DEPTH + 1) // 2
N_ODD = DEPTH // 2
EPS = 1e-6

kernel_name = "hybrid_diffattn_natten_hyena_encoder"


def rms_norm(x, g):
    xf = x.astype(jnp.float32)
    y = xf * lax.rsqrt(jnp.mean(xf * xf, axis=-1, keepdims=True) + EPS)
    return (y * g.astype(jnp.float32)).astype(x.dtype)


def t5_bucket(rel):
    nb = T5_BUCKETS // 2
    max_exact = nb // 2
    ret = jnp.where(rel > 0, nb, 0)
    n = jnp.abs(rel)
    nf = jnp.maximum(n, 1).astype(jnp.float32)
    large = max_exact + (jnp.log(nf / max_exact) / math.log(T5_MAX_DIST / max_exact)
                         * (nb - max_exact)).astype(jnp.int32)
    large = jnp.minimum(large, nb - 1)
    return ret + jnp.where(n < max_exact, n, large)


def diff_attention(q1, q2, k1, k2, v, lam, rel_table):
    B, L, H, dk = q1.shape
    scale = dk ** -0.5
    n_blk = L // Q_BLOCK
    kpos = jnp.arange(L, dtype=jnp.int32)

    def block(i):
        qs = i * Q_BLOCK
        q1b = lax.dynamic_slice_in_dim(q1, qs, Q_BLOCK, axis=1)
        q2b = lax.dynamic_slice_in_dim(q2, qs, Q_BLOCK, axis=1)
        qpos = qs + jnp.arange(Q_BLOCK, dtype=jnp.int32)
        bias = rel_table[t5_bucket(kpos[None, :] - qpos[:, None])]
        bias = jnp.transpose(bias, (2, 0, 1)).astype(jnp.float32)
        s1 = jnp.einsum('bqhd,bkhd->bhqk', q1b, k1).astype(jnp.float32) * scale + bias
        s2 = jnp.einsum('bqhd,bkhd->bhqk', q2b, k2).astype(jnp.float32) * scale + bias
        a = jax.nn.softmax(s1, axis=-1) - lam * jax.nn.softmax(s2, axis=-1)
        return jnp.einsum('bhqk,bkhd->bqhd', a.astype(v.dtype), v)

    out = lax.map(block, jnp.arange(n_blk, dtype=jnp.int32))
    return jnp.transpose(out, (1, 0, 2, 3, 4)).reshape(B, L, H, v.shape[-1])


def neighbourhood_attention(q, k, v, na_table):
    B, L, H, d = q.shape
    rows = L // GRID_W
    kh = min(NA_ROWS, rows)
    kw = NA_COLS
    n_keys = kh * kw
    scale = d ** -0.5
    cols = jnp.arange(GRID_W, dtype=jnp.int32)
    c_start = jnp.clip(cols - kw // 2, 0, GRID_W - kw)
    key_cols = c_start[:, None] + jnp.arange(kw, dtype=jnp.int32)
    dc = key_cols - cols[:, None] + (NA_COLS - 1)
    qg = jnp.transpose(q.reshape(B, rows, GRID_W, H, d), (1, 0, 2, 3, 4))

    def row_block(args):
        r, q_row = args
        r_start = jnp.clip(r - kh // 2, 0, rows - kh)
        key_rows = r_start + jnp.arange(kh, dtype=jnp.int32)
        idx = (key_rows[None, :, None] * GRID_W + key_cols[:, None, :]).reshape(GRID_W, n_keys)
        kb = jnp.take(k, idx, axis=1)
        vb = jnp.take(v, idx, axis=1)
        dr = key_rows - r + (NA_ROWS - 1)
        bias = na_table[:, dr[None, :, None], dc[:, None, :]]
        bias = bias.reshape(H, GRID_W, n_keys).astype(jnp.float32)
        s = jnp.einsum('bwhd,bwkhd->bhwk', q_row, kb).astype(jnp.float32) * scale + bias
        p = jax.nn.softmax(s, axis=-1)
        return jnp.einsum('bhwk,bwkhd->bwhd', p.astype(v.dtype), vb)

    out = lax.map(row_block, (jnp.arange(rows, dtype=jnp.int32), qg))
    return jnp.transpose(out, (1, 0, 2, 3, 4)).reshape(B, L, H, d)


def even_mixer(u, w_in, w_out, lam_vec, subln_g, na_table, rel_table, layer_idx):
    B, L, _ = u.shape
    z = u @ w_in
    qa, ka, va, qb, kb, vb = jnp.split(z, 6, axis=-1)
    qa = qa.reshape(B, L, A_HEADS, 2, A_QK_DIM)
    ka = ka.reshape(B, L, A_HEADS, 2, A_QK_DIM)
    va = va.reshape(B, L, A_HEADS, A_V_DIM)
    lam_init = 0.8 - 0.6 * math.exp(-0.3 * layer_idx)
    lv = lam_vec.astype(jnp.float32)
    lam = jnp.exp(jnp.sum(lv[0] * lv[1])) - jnp.exp(jnp.sum(lv[2] * lv[3])) + lam_init
    oa = diff_attention(qa[..., 0, :], qa[..., 1, :], ka[..., 0, :], ka[..., 1, :], va, lam, rel_table)
    oa = (rms_norm(oa, subln_g) * (1.0 - lam_init)).reshape(B, L, A_WIDTH)
    ob = neighbourhood_attention(qb.reshape(B, L, B_HEADS, B_HEAD_DIM),
                                 kb.reshape(B, L, B_HEADS, B_HEAD_DIM),
                                 vb.reshape(B, L, B_HEADS, B_HEAD_DIM), na_table)
    ob = ob.reshape(B, L, B_WIDTH)
    return jnp.concatenate([oa, ob], axis=-1) @ w_out


def hyena_filter(L, w1, b1, freq, w2, b2, w3):
    t = jnp.linspace(0.0, 1.0, L, dtype=jnp.float32)[:, None]
    bands = (FILTER_EMB - 1) // 2
    w = 2.0 * math.pi * jnp.arange(L, dtype=jnp.float32) / L
    f = jnp.linspace(1e-4, bands - 1, bands, dtype=jnp.float32)
    ang = w[:, None] * f[None, :]
    feats = jnp.concatenate([t, jnp.cos(ang), -jnp.sin(ang)], axis=-1)
    hdn = jnp.sin(freq[0] * (feats @ w1 + b1))
    hdn = jnp.sin(freq[1] * (hdn @ w2 + b2))
    h = (hdn @ w3).astype(jnp.float32)
    min_decay = math.log(FILTER_TARGET) / FAST_DECAY_PCT
    max_decay = math.log(FILTER_TARGET) / SLOW_DECAY_PCT
    deltas = jnp.abs(jnp.linspace(min_decay, max_decay, C_WIDTH, dtype=jnp.float32))
    decay = jnp.exp(-t * deltas[None, :])
    h_fwd = h[:, :C_WIDTH] * decay
    h_bwd = h[:, C_WIDTH:] * decay
    zero = jnp.zeros((1, C_WIDTH), jnp.float32)
    return jnp.concatenate([h_fwd, zero, h_bwd[:0:-1]], axis=0)


def hyena_mixer(u, w_in, conv_w, conv_b, w1, b1, freq, w2, b2, w3, skip, w_out):
    B, L, _ = u.shape
    z = u @ w_in
    zp = jnp.pad(z, ((0, 0), (1, 1), (0, 0)))
    z = zp[:, :-2] * conv_w[0] + zp[:, 1:-1] * conv_w[1] + zp[:, 2:] * conv_w[2] + conv_b
    x0, x1, v = jnp.split(z, 3, axis=-1)
    v = v * x1
    h = hyena_filter(L, w1, b1, freq, w2, b2, w3)
    V = jnp.fft.rfft(v.astype(jnp.float32), n=2 * L, axis=1)
    Hf = jnp.fft.rfft(h, axis=0)
    y = jnp.fft.irfft(V * Hf[None], n=2 * L, axis=1)[:, :L].astype(u.dtype)
    y = (y + v * skip) * x0
    return y @ w_out


def swiglu(u, w_in, w_out):
    g, up = jnp.split(u @ w_in, 2, axis=-1)
    return (jax.nn.silu(g) * up) @ w_out


def trunk(x, p, W):
    h = x
    for i in range(DEPTH):
        j = i // 2
        hn = rms_norm(h, W['norm_mix'][i])
        if i % 2 == 0:
            mix = even_mixer(hn, W['ab_w_in'][j], W['ab_w_out'][j], W['diff_lambda'][j],
                             W['diff_subln'][j], W['na_bias'][j], W['rel_bias_table'], i)
        else:
            mix = hyena_mixer(hn, W['c_w_in'][j], W['c_conv_w'][j], W['c_conv_b'][j],
                              W['c_filt_w1'][j], W['c_filt_b1'][j], W['c_filt_freq'][j],
                              W['c_filt_w2'][j], W['c_filt_b2'][j], W['c_filt_w3'][j],
                              W['c_skip'][j], W['c_w_out'][j])
        h = h + mix
        h = h + swiglu(rms_norm(h, W['norm_ffn'][i]), W['ffn_w_in'][i], W['ffn_w_out'][i])
        gate = jax.nn.sigmoid(rms_norm(h, W['norm_ple'][i]) @ W['ple_w_gate'][i])
        h = h + (p[i] @ W['ple_w_proj'][i]) * gate
    return rms_norm(h, W['final_norm'])


def setup_inputs(seed: int = 0) -> dict:
    key = jax.random.key(seed)
    keys = iter(jax.random.split(key, 32))

    def nrm(shape, scale):
        return scale * jax.random.normal(next(keys), shape, jnp.float32)

    def gain(shape):
        return 1.0 + nrm(shape, 0.02)

    return {
        "x_prompt": nrm((BATCH, SEQ, D_MODEL), 1.0),
        "x_sample": nrm((DEC_BATCH, DEC_SEQ, D_MODEL), 1.0),
        "p_prompt": nrm((DEPTH, BATCH, SEQ, PLE_DIM), 1.0),
        "p_sample": nrm((DEPTH, DEC_BATCH, DEC_SEQ, PLE_DIM), 1.0),
        "rel_bias_table": nrm((T5_BUCKETS, A_HEADS), 0.1),
        "norm_mix": gain((DEPTH, D_MODEL)),
        "norm_ffn": gain((DEPTH, D_MODEL)),
        "norm_ple": gain((DEPTH, D_MODEL)),
        "final_norm": gain((D_MODEL,)),
        "ab_w_in": nrm((N_EVEN, D_MODEL, AB_IN), D_MODEL ** -0.5),
        "ab_w_out": nrm((N_EVEN, AB_OUT, D_MODEL), AB_OUT ** -0.5),
        "diff_lambda": nrm((N_EVEN, 4, A_QK_DIM), 0.1),
        "diff_subln": gain((N_EVEN, A_V_DIM)),
        "na_bias": nrm((N_EVEN, B_HEADS, 2 * NA_ROWS - 1, 2 * NA_COLS - 1), 0.1),
        "c_w_in": nrm((N_ODD, D_MODEL, 3 * C_WIDTH), D_MODEL ** -0.5),
        "c_conv_w": nrm((N_ODD, SHORT_CONV, 3 * C_WIDTH), SHORT_CONV ** -0.5),
        "c_conv_b": nrm((N_ODD, 3 * C_WIDTH), 0.02),
        "c_filt_w1": nrm((N_ODD, FILTER_EMB, FILTER_HIDDEN), FILTER_EMB ** -0.5),
        "c_filt_b1": nrm((N_ODD, FILTER_HIDDEN), 0.02),
        "c_filt_freq": gain((N_ODD, 2, FILTER_HIDDEN)),
        "c_filt_w2": nrm((N_ODD, FILTER_HIDDEN, FILTER_HIDDEN), FILTER_HIDDEN ** -0.5),
        "c_filt_b2": nrm((N_ODD, FILTER_HIDDEN), 0.02),
        "c_filt_w3": nrm((N_ODD, FILTER_HIDDEN, 2 * C_WIDTH), 0.05 * FILTER_HIDDEN ** -0.5),
        "c_skip": nrm((N_ODD, C_WIDTH), 0.5),
        "c_w_out": nrm((N_ODD, C_WIDTH, D_MODEL), C_WIDTH ** -0.5),
        "ffn_w_in": nrm((DEPTH, D_MODEL, 2 * D_FF), D_MODEL ** -0.5),
        "ffn_w_out": nrm((DEPTH, D_FF, D_MODEL), D_FF ** -0.5),
        "ple_w_proj": nrm((DEPTH, PLE_DIM, D_MODEL), PLE_DIM ** -0.5),
        "ple_w_gate": nrm((DEPTH, D_MODEL, D_MODEL), D_MODEL ** -0.5),
    }


def reference(x_prompt, x_sample, p_prompt, p_sample, rel_bias_table, norm_mix, norm_ffn,
              norm_ple, final_norm, ab_w_in, ab_w_out, diff_lambda, diff_subln, na_bias,
              c_w_in, c_conv_w, c_conv_b, c_filt_w1, c_filt_b1, c_filt_freq, c_filt_w2,
              c_filt_b2, c_filt_w3, c_skip, c_w_out, ffn_w_in, ffn_w_out, ple_w_proj, ple_w_gate):
    W = dict(rel_bias_table=rel_bias_table, norm_mix=norm_mix, norm_ffn=norm_ffn,
             norm_ple=norm_ple, final_norm=final_norm, ab_w_in=ab_w_in, ab_w_out=ab_w_out,
             diff_lambda=diff_lambda, diff_subln=diff_subln, na_bias=na_bias,
             c_w_in=c_w_in, c_conv_w=c_conv_w, c_conv_b=c_conv_b, c_filt_w1=c_filt_w1,
             c_filt_b1=c_filt_b1, c_filt_freq=c_filt_freq, c_filt_w2=c_filt_w2,
             c_filt_b2=c_filt_b2, c_filt_w3=c_filt_w3, c_skip=c_skip, c_w_out=c_w_out,
             ffn_w_in=ffn_w_in, ffn_w_out=ffn_w_out, ple_w_proj=ple_w_proj,
             ple_w_gate=ple_w_gate)
    y_prompt = trunk(x_prompt, p_prompt, W)
    y_sample = trunk(x_sample, p_sample, W)
    return (y_prompt, y_sample)
```

```python
import math
from contextlib import ExitStack
import numpy as np
import ml_dtypes
import concourse.bass as bass
import concourse.mybir as mybir
from concourse.bass_utils import run_bass_kernel_spmd

F32 = mybir.dt.float32
BF16 = mybir.dt.bfloat16
I32 = mybir.dt.int32
AF = mybir.ActivationFunctionType
ALU = mybir.AluOpType
AX = mybir.AxisListType

D = 2048
NCH = 16
DFF = 5632
T = 512
EPS = 1e-6
NEG = -30000.0


class Tl:
    __slots__ = ("w", "r", "p")

    def __init__(self):
        self.w = {}
        self.r = {}
        self.p = {}


class Sched:
    ENG = ("pe", "act", "dve", "pool", "sp")

    def __init__(self, nc, stack, n_dma_sems=8):
        self.nc = nc
        self.sems = []
        self.cnt = []
        self.esem = {}
        self.ops = {e: [] for e in self.ENG}
        self.seen = {e: {} for e in self.ENG}
        for e in self.ENG:
            self.esem[e] = self._new_sem(stack, "s_" + e)
        self.dpool = {}
        self.dnext = {}
        for q in ("sp", "pool", "act"):
            self.dpool[q] = [self._new_sem(stack, "d_%s%d" % (q, i)) for i in range(n_dma_sems)]
            self.dnext[q] = 0

    def _new_sem(self, stack, name):
        s = stack.enter_context(self.nc.semaphore(name))
        self.sems.append(s)
        self.cnt.append(0)
        return len(self.sems) - 1

    def _waits(self, eng, reads, writes, extra=(), addw=()):
        need = {}
        for t in reads:
            for d in (t.w, t.p):
                for s, v in d.items():
                    if need.get(s, 0) < v:
                        need[s] = v
        for t in writes:
            for d in (t.w, t.r, t.p):
                for s, v in d.items():
                    if need.get(s, 0) < v:
                        need[s] = v
        for t in addw:
            for s, v in t.p.items():
                if need.get(s, 0) < v:
                    need[s] = v
        for s, v in extra:
            if need.get(s, 0) < v:
                need[s] = v
        seen = self.seen[eng]
        own = self.esem[eng]
        out = []
        for s, v in need.items():
            if eng == "pe" and s == own:
                continue
            if seen.get(s, 0) >= v:
                continue
            seen[s] = v
            out.append((s, v))
        return out

    def _mark(self, ev, reads, writes, add):
        s, v = ev
        for t in reads:
            if t.r.get(s, 0) < v:
                t.r[s] = v
        for t in writes:
            if add:
                t.w[s] = v
            else:
                t.w = {s: v}
                t.r = {}
                t.p = {}

    def epoch(self, t):
        p = dict(t.p)
        for d in (t.w, t.r):
            for s, v in d.items():
                if p.get(s, 0) < v:
                    p[s] = v
        t.p = p
        t.w = {}
        t.r = {}

    def op(self, eng, fn, reads=(), writes=(), add=False):
        waits = self._waits(eng, reads, () if add else writes, (), writes if add else ())
        s = self.esem[eng]
        self.cnt[s] += 1
        ev = (s, self.cnt[s])
        self.ops[eng].append((waits, fn, (s, 1)))
        self._mark(ev, reads, writes, add)

    def grp(self, eng, fns, reads=(), writes=()):
        waits = self._waits(eng, reads, writes)
        s = self.esem[eng]
        self.cnt[s] += 1
        ev = (s, self.cnt[s])
        n = len(fns)
        for i, fn in enumerate(fns):
            self.ops[eng].append((waits if i == 0 else (), fn, (s, 1) if i == n - 1 else None))
        self._mark(ev, reads, writes, False)

    def dma(self, q, fn, reads=(), writes=(), add=False):
        pool = self.dpool[q]
        i = self.dnext[q]
        self.dnext[q] = (i + 1) % len(pool)
        s = pool[i]
        extra = [(s, self.cnt[s])] if self.cnt[s] > 0 else []
        waits = self._waits(q, reads, () if add else writes, extra, writes if add else ())
        self.cnt[s] += 16
        ev = (s, self.cnt[s])
        self.ops[q].append((waits, fn, (s, 16)))
        self._mark(ev, reads, writes, add)

    def barrier(self):
        allv = [(s, v) for s, v in enumerate(self.cnt) if v > 0]
        for e in self.ENG:
            seen = self.seen[e]
            w = []
            for s, v in allv:
                if seen.get(s, 0) < v:
                    seen[s] = v
                    w.append((s, v))
            if w:
                self.ops[e].append((w, None, None))

    def emit(self):
        nc = self.nc
        sems = self.sems
        with nc.Block() as b0:
            @b0.vector
            def _(v):
                for s in sems:
                    v.sem_clear(s)
        with nc.Block() as blk:
            def run(e):
                def body(h):
                    for waits, fn, inc in self.ops[e]:
                        for s, v in waits:
                            h.wait_ge(sems[s], v)
                        if fn is not None:
                            ins = fn(h)
                            if inc is not None:
                                ins.then_inc(sems[inc[0]], inc[1])
                return body
            blk.tensor(run("pe"))
            blk.scalar(run("act"))
            blk.vector(run("dve"))
            blk.gpsimd(run("pool"))
            blk.sync(run("sp"))


def t5_bucket_np(rel):
    nb = 16
    max_exact = 8
    ret = np.where(rel > 0, nb, 0)
    n = np.abs(rel)
    nf = np.maximum(n, 1).astype(np.float32)
    large = max_exact + (np.log(nf / np.float32(max_exact)) / np.float32(math.log(128 / max_exact))
                         * np.float32(nb - max_exact)).astype(np.int32)
    large = np.minimum(large, nb - 1)
    return ret + np.where(n < max_exact, n, large)


def t5_tiles(rel_table):
    kk = np.arange(128)[:, None, None]
    j = np.arange(6)[None, :, None]
    qq = np.arange(512)[None, None, :]
    rel = 128 * (j - 1) + kk - qq
    b = t5_bucket_np(rel.astype(np.int32))
    tiles = np.ascontiguousarray(np.transpose(rel_table[b], (3, 0, 1, 2))).astype(np.float32)
    far = np.stack([rel_table[15], rel_table[31]], axis=-1)
    far = np.ascontiguousarray(np.broadcast_to(far[:, None, :], (8, 128, 2))).astype(np.float32)
    return tiles, far


def na_tiles(na_table):
    out = np.empty((3, 8, 128, 8, 512), np.float32)
    rows = 64
    for ty, m in enumerate((0, 2, rows // 8 - 1)):
        ws = int(np.clip(8 * m - 4, 0, rows - 16))
        kk = np.arange(128)[:, None, None]
        j = np.arange(8)[None, :, None]
        qq = np.arange(512)[None, None, :]
        krow = ws + 2 * j + kk // 64
        kcol = kk % 64
        qrow = 8 * m + qq // 64
        qcol = qq % 64
        rs = np.clip(qrow - 4, 0, rows - 8)
        cs = np.clip(qcol - 8, 0, 64 - 16)
        valid = (krow >= rs) & (krow < rs + 8) & (kcol >= cs) & (kcol < cs + 16)
        dr = np.clip(krow - qrow + 7, 0, 14)
        dc = np.clip(kcol - qcol + 15, 0, 30)
        dr, dc, valid = np.broadcast_arrays(dr, dc, valid)
        g = na_table[:, dr, dc]
        out[ty] = np.where(valid[None], g, np.float32(NEG))
    return out


def dft_mats(L):
    N = 2 * L
    Lh = L // 2
    tau = np.arange(Lh, dtype=np.int64)[:, None]
    f = np.arange(Lh, dtype=np.int64)[None, :]
    out = []
    for tt_ in (2 * tau, 2 * tau + 1):
        m = (tt_ * f) % N
        ang = m.astype(np.float64) * (2.0 * np.pi / N)
        out.append((np.cos(ang).astype(ml_dtypes.bfloat16), np.sin(ang).astype(ml_dtypes.bfloat16)))
    (ce, se), (co, so) = out
    return [ce, se, co, so, np.ascontiguousarray(co.T), np.ascontiguousarray(so.T)]


def filt_consts(L):
    t = np.linspace(0.0, 1.0, L, dtype=np.float32)[:, None]
    bands = 16
    w = (2.0 * math.pi * np.arange(L, dtype=np.float32) / L).astype(np.float32)
    f = np.linspace(1e-4, bands - 1, bands, dtype=np.float32)
    ang = w[:, None] * f[None, :]
    feats = np.concatenate([t, np.cos(ang), -np.sin(ang)], axis=-1).astype(np.float32)
    nh = L // 256
    tv = -t[:, 0]
    ev = tv[0::2].reshape(nh, 128).T
    od = tv[1::2].reshape(nh, 128).T
    negt = np.concatenate([ev, od], axis=1)
    return np.ascontiguousarray(feats.T), np.ascontiguousarray(negt.astype(np.float32))


def decay_deltas():
    min_decay = math.log(1e-2) / 0.3
    max_decay = math.log(1e-2) / 1.5
    return np.abs(np.linspace(min_decay, max_decay, D, dtype=np.float32)).astype(np.float32)


W2D = {
    "ab_in": (D, 6144), "ab_out": (D, D), "c_in": (D, 6144), "c_out": (D, D),
    "ffn_in0": (D, 2 * DFF), "ffn_in1": (D, 2 * DFF), "ffn_out0": (DFF, D), "ffn_out1": (DFF, D),
    "gate0": (D, D), "gate1": (D, D), "proj0": (256, D), "proj1": (256, D),
}


def build_program(SEQS, dbg=()):
    nc = bass.Bass("TRN2", target_bir_lowering=False)
    NT = sum(SEQS)
    NTT = NT // T
    seq_off = [sum(SEQS[:i]) for i in range(len(SEQS))]
    zoff = [sum(l + 2 for l in SEQS[:i]) for i in range(len(SEQS))]
    NTZ = sum(l + 2 for l in SEQS)
    LS = sorted(set(SEQS))

    def din(name, shape, dt=F32):
        return nc.dram_tensor(name, list(shape), dt, kind="ExternalInput").ap()

    def dscr(name, shape, dt):
        kind = "ExternalOutput" if name in dbg else "Internal"
        return nc.dram_tensor(name, list(shape), dt, kind=kind).ap()

    xs = din("xs", [NT, D])
    pp = din("pp", [2, NT, 256])
    w_src = {
        "ab_in": din("ab_w_in", [1, D, 6144])[0], "ab_out": din("ab_w_out", [1, D, D])[0],
        "c_in": din("c_w_in", [1, D, 6144])[0], "c_out": din("c_w_out", [1, D, D])[0],
    }
    ffn_in = din("ffn_w_in", [2, D, 2 * DFF])
    ffn_out = din("ffn_w_out", [2, DFF, D])
    gate = din("ple_w_gate", [2, D, D])
    proj = din("ple_w_proj", [2, 256, D])
    for i in range(2):
        w_src["ffn_in%d" % i] = ffn_in[i]
        w_src["ffn_out%d" % i] = ffn_out[i]
        w_src["gate%d" % i] = gate[i]
        w_src["proj%d" % i] = proj[i]
    norm_mix = din("norm_mix", [2, D])
    norm_ffn = din("norm_ffn", [2, D])
    norm_ple = din("norm_ple", [2, D])
    final_norm = din("final_norm", [D])
    diff_lambda = din("diff_lambda", [1, 4, 64])
    diff_subln = din("diff_subln", [1, 128])
    c_conv_w = din("c_conv_w", [1, 3, 6144])
    c_conv_b = din("c_conv_b", [1, 6144])
    f_w1 = din("c_filt_w1", [1, 33, 64])
    f_b1 = din("c_filt_b1", [1, 64])
    f_freq = din("c_filt_freq", [1, 2, 64])
    f_w2 = din("c_filt_w2", [1, 64, 64])
    f_b2 = din("c_filt_b2", [1, 64])
    f_w3 = din("c_filt_w3", [1, 64, 2 * D])
    c_skip = din("c_skip", [1, D])
    ident_d = din("k_ident", [128, 128])
    t5t_d = din("k_t5t", [8, 128, 6, 512])
    t5f_d = din("k_t5f", [8, 128, 2])
    nat_d = din("k_nat", [3, 8, 128, 8, 512])
    delta_d = din("k_delta", [128, D])
    alt_d = din("k_alt", [128, 512])
    dftm = {L: [din("k_dft%d_%d" % (L, k), [L // 2, L // 2], BF16) for k in range(6)] for L in LS}
    feat_d = {L: din("k_feat%d" % L, [33, L]) for L in LS}
    negt_d = {L: din("k_negt%d" % L, [128, L // 128]) for L in LS}
    wn_d = {L: din("k_wn%d" % L, [128, L // 256]) for L in LS}
    y_out = nc.dram_tensor("y", [NT, D], F32, kind="ExternalOutput").ap()
    wb = {k: dscr("wb_" + k, shp, BF16) for k, shp in W2D.items()}
    hscr = dscr("hscr", [D, NT], F32)
    qkscr = dscr("qkscr", [4096, NT], BF16)
    vscr_a = dscr("vscr_a", [NT, 2048], BF16)
    oT = dscr("oT", [D, NT], BF16)
    zscr = dscr("zscr", [6144, NTZ], F32)
    vfm = dscr("vfm", [D, NT], F32)
    Hscr = {i: dscr("Hscr%d" % i, [4, SEQS[i] // 2 + 128, D], F32) for i in range(len(SEQS))}

    with ExitStack() as st:
        S = Sched(nc, st)

        uq = {"n": 0}

        def sb(stack, name, shape, dt):
            uq["n"] += 1
            return stack.enter_context(nc.sbuf_tensor("%s_%d" % (name, uq["n"]), list(shape), dt))

        psb = [st.enter_context(nc.psum_tensor("ps%d" % i, [128, 512], F32)) for i in range(8)]
        t_ps = [Tl() for _ in range(8)]
        rr = {"k": 0, "ev": 0}
        cast_late = []

        def bank(lo=0, hi=8):
            b = lo + rr["k"] % (hi - lo)
            rr["k"] += 1
            return b

        def evac_eng():
            rr["ev"] += 1
            return "act" if rr["ev"] % 2 else "dve"

        def copy_op(eng, out, in_, reads, writes, add=False):
            if eng == "act":
                S.op("act", lambda h: h.copy(out=out, in_=in_), reads=reads, writes=writes, add=add)
            else:
                S.op(eng, lambda h: h.tensor_copy(out=out, in_=in_), reads=reads, writes=writes, add=add)

        ident = sb(st, "ident", [128, 128], F32)
        identb = sb(st, "identb", [128, 128], BF16)
        ones = sb(st, "ones", [128, 128], BF16)
        cst = sb(st, "cst", [128, 8], F32)
        gains = sb(st, "gains", [128, 7, NCH], F32)
        lamt = sb(st, "lamt", [128, 4], F32)
        subg = sb(st, "subg", [128, 1], F32)
        t_const = Tl()
        S.dma("sp", lambda h: h.dma_start(out=ident[:], in_=ident_d[:, :]), writes=[t_const], add=True)
        S.op("pool", lambda h: h.memset(ones[:], 1.0), writes=[t_const], add=True)
        S.op("pool", lambda h: h.memset(cst[:, 0:1], EPS), writes=[t_const], add=True)
        S.op("pool", lambda h: h.memset(cst[:, 1:2], -3.14159), writes=[t_const], add=True)
        S.op("pool", lambda h: h.memset(cst[:, 2:3], 0.0), writes=[t_const], add=True)
        S.op("pool", lambda h: h.memset(cst[:, 3:4], 1.0), writes=[t_const], add=True)
        S.barrier()
        S.op("pool", lambda h: h.memset(cst[0:1, 3:4], 0.0), writes=[t_const], add=True)
        S.op("dve", lambda h: h.tensor_copy(out=identb[:], in_=ident[:]), reads=[t_const], writes=[t_const], add=True)
        gsrc = [norm_mix[0], norm_mix[1], norm_ffn[0], norm_ffn[1], norm_ple[0], norm_ple[1], final_norm]
        for i, g in enumerate(gsrc):
            S.dma("sp", lambda h, i=i, g=g: h.dma_start(out=gains[:, i, :], in_=g.rearrange("(c p) -> p c", p=128),
                                                        allow_slow_non_contiguous=True), writes=[t_const], add=True)
        S.dma("sp", lambda h: h.dma_start(out=subg[:], in_=diff_subln[0].rearrange("(p o) -> p o", o=1),
                                          allow_slow_non_contiguous=True), writes=[t_const], add=True)
        with ExitStack() as ph:
            lv = sb(ph, "lv", [128, 4, 64], F32)
            lp = sb(ph, "lp", [128, 2, 64], F32)
            ls = sb(ph, "ls", [128, 2], F32)
            le = sb(ph, "le", [128, 2], F32)
            t_lv = Tl()
            lam_src = bass.AP(diff_lambda.tensor, diff_lambda.offset, [[0, 128], [64, 4], [1, 64]])
            S.dma("sp", lambda h: h.dma_start(out=lv[:], in_=lam_src), writes=[t_lv])
            S.op("dve", lambda h: h.tensor_tensor(out=lp[:, 0, :], in0=lv[:, 0, :], in1=lv[:, 1, :], op=ALU.mult), reads=[t_lv], writes=[t_lv])
            S.op("dve", lambda h: h.tensor_tensor(out=lp[:, 1, :], in0=lv[:, 2, :], in1=lv[:, 3, :], op=ALU.mult), reads=[t_lv], writes=[t_lv])
            S.op("dve", lambda h: h.tensor_reduce(out=ls[:], in_=lp[:], axis=AX.X, op=ALU.add), reads=[t_lv], writes=[t_lv])
            S.op("act", lambda h: h.activation(out=le[:], in_=ls[:], func=AF.Exp), reads=[t_lv], writes=[t_lv])
            S.op("dve", lambda h: h.tensor_tensor(out=lamt[:, 0:1], in0=le[:, 0:1], in1=le[:, 1:2], op=ALU.subtract), reads=[t_lv], writes=[t_lv])
            S.op("dve", lambda h: h.tensor_scalar(out=lamt[:, 0:1], in0=lamt[:, 0:1], scalar1=0.2, scalar2=None, op0=ALU.add), reads=[t_lv], writes=[t_lv])
            S.op("dve", lambda h: h.tensor_scalar(out=lamt[:, 1:2], in0=lamt[:, 0:1], scalar1=-1.0, scalar2=None, op0=ALU.mult), reads=[t_lv], writes=[t_lv])
            def cast_thunks(names, CAST_ROWS=512):
                out = []
                for k in names:
                    K, N = W2D[k]
                    r0 = 0
                    while r0 < K:
                        nr = min(CAST_ROWS, K - r0)
                        out.append(lambda k=k, r0=r0, nr=nr: S.dma("pool", lambda h: h.dma_start(out=wb[k][r0:r0 + nr, :], in_=w_src[k][r0:r0 + nr, :]),
                                                                   writes=[t_const], add=True))
                        r0 += nr
                return out

            def cast_weights(names):
                for th in cast_thunks(names):
                    th()
            cast_weights(["ab_in"])
            S.barrier()

        if "stop0" in dbg:
            S.emit()
            return nc
        def rmsnorm(hT, t_hT, hn, t_hn, gi, sq, t_sq, rs, rstd, t_rs, tn=None, t_tn=None):
            b = bank()
            for c in range(NCH):
                i = c % 4
                if c % 2 == 0:
                    S.op("act", lambda h, c=c, i=i: h.activation(out=sq[i][:], in_=hT[:, c, :], func=AF.Square), reads=[t_hT], writes=[t_sq[i]])
                else:
                    S.op("pool", lambda h, c=c, i=i: h.tensor_tensor(out=sq[i][:], in0=hT[:, c, :], in1=hT[:, c, :], op=ALU.mult), reads=[t_hT], writes=[t_sq[i]])
                S.op("pe", lambda h, c=c, i=i, b=b: h.matmul(psb[b][:], ones[:], sq[i][:], start=(c == 0), stop=(c == NCH - 1)),
                     reads=[t_const, t_sq[i]], writes=[t_ps[b]])
            S.op("act", lambda h, b=b: h.activation(out=rs[:], in_=psb[b][:], func=AF.Sqrt, bias=cst[:, 0:1], scale=1.0 / D), reads=[t_ps[b], t_const], writes=[t_rs])
            S.op("dve", lambda h: h.reciprocal(out=rstd[:], in_=rs[:]), reads=[t_rs], writes=[t_rs])
            if hn is not None:
                S.epoch(t_hn)
                for c in range(NCH):
                    if c % 2 == 0 or tn is None:
                        S.op("dve", lambda h, c=c: h.scalar_tensor_tensor(out=hn[:, c, :], in0=hT[:, c, :], scalar=gains[:, gi, c:c + 1], in1=rstd[:],
                                                                          op0=ALU.mult, op1=ALU.mult), reads=[t_hT, t_const, t_rs], writes=[t_hn], add=True)
                    else:
                        k = (c // 2) % len(tn)
                        S.op("pool", lambda h, c=c, k=k: h.tensor_tensor(out=tn[k][:], in0=hT[:, c, :], in1=rstd[:], op=ALU.mult), reads=[t_hT, t_rs], writes=[t_tn[k]])
                        S.op("act", lambda h, c=c, k=k: h.activation(out=hn[:, c, :], in_=tn[k][:], func=AF.Copy, scale=gains[:, gi, c:c + 1]), reads=[t_tn[k], t_const], writes=[t_hn], add=True)

        def rmsnorm_pre(hT, t_hT, hn, t_hn, gi):
            S.epoch(t_hn)
            for c in range(NCH):
                if c % 2 == 0:
                    S.op("dve", lambda h, c=c: h.tensor_scalar(out=hn[:, c, :], in0=hT[:, c, :], scalar1=gains[:, gi, c:c + 1], scalar2=None, op0=ALU.mult),
                         reads=[t_hT, t_const], writes=[t_hn], add=True)
                else:
                    S.op("act", lambda h, c=c: h.activation(out=hn[:, c, :], in_=hT[:, c, :], func=AF.Copy, scale=gains[:, gi, c:c + 1]),
                         reads=[t_hT, t_const], writes=[t_hn], add=True)

        def rmsnorm_stats(hT, t_hT, sq, t_sq, rs, rstd, t_rs):
            b = bank()
            for c in range(NCH):
                i = c % 4
                if c % 2 == 0:
                    S.op("act", lambda h, c=c, i=i: h.activation(out=sq[i][:], in_=hT[:, c, :], func=AF.Square), reads=[t_hT], writes=[t_sq[i]])
                else:
                    S.op("pool", lambda h, c=c, i=i: h.tensor_tensor(out=sq[i][:], in0=hT[:, c, :], in1=hT[:, c, :], op=ALU.mult), reads=[t_hT], writes=[t_sq[i]])
                S.op("pe", lambda h, c=c, i=i, b=b: h.matmul(psb[b][:], ones[:], sq[i][:], start=(c == 0), stop=(c == NCH - 1)),
                     reads=[t_const, t_sq[i]], writes=[t_ps[b]])
            S.op("act", lambda h, b=b: h.activation(out=rs[:], in_=psb[b][:], func=AF.Sqrt, bias=cst[:, 0:1], scale=1.0 / D), reads=[t_ps[b], t_const], writes=[t_rs])
            S.op("dve", lambda h: h.reciprocal(out=rstd[:], in_=rs[:]), reads=[t_rs], writes=[t_rs])

        class WStream:
            def __init__(self, stack, nbuf=3):
                self.buf = [sb(stack, "wbuf%d" % i, [128, 16 * 512], BF16) for i in range(nbuf)]
                self.tl = [Tl() for _ in range(nbuf)]
                self.k = 0

            def load(self, specs):
                i = self.k % len(self.buf)
                self.k += 1
                kc = specs[0][1]
                W = sum(s[2] for s in specs)
                view = self.buf[i][:, 0:kc * W].rearrange("p (c n) -> p c n", n=W)
                o = 0
                for n, (ap, kc_, ncols) in enumerate(specs):
                    S.dma("sp", lambda h, ap=ap, o=o, ncols=ncols, view=view: h.dma_start(out=view[:, :, o:o + ncols], in_=ap.rearrange("(c p) n -> p c n", p=128)),
                          writes=[self.tl[i]], add=(n > 0))
                    o += ncols
                return view, self.tl[i]

        def transpose_in(src_tm, t_src, dst_fm, t_dst, nchunk, dt_ident, eng_alt=True):
            for c in range(nchunk):
                b = bank()
                fns = [lambda h, c=c, tb=tb, b=b: h.transpose(psb[b][:, tb * 128:(tb + 1) * 128], src_tm[:, tb, c * 128:(c + 1) * 128], dt_ident[:]) for tb in range(4)]
                S.grp("pe", fns, reads=[t_src, t_const], writes=[t_ps[b]])
                copy_op(evac_eng(), dst_fm[:, c, :], psb[b][:], [t_ps[b]], [t_dst], add=True)

        with ExitStack() as ph:
            WS = WStream(ph)
            xt = sb(ph, "xt", [128, 4, D], F32)
            hT = sb(ph, "hT", [128, NCH, T], F32)
            hn = sb(ph, "hn", [128, NCH, T], BF16)
            sq = [sb(ph, "sq%d" % i, [128, T], BF16) for i in range(4)]
            rs = sb(ph, "rs", [128, T], F32)
            rstd = sb(ph, "rstd", [128, T], F32)
            stg = [sb(ph, "stg%d" % i, [128, 4, T], BF16) for i in range(2)]
            tn1 = [sb(ph, "tn1_%d" % i, [128, T], F32) for i in range(2)]
            t_tn1 = [Tl(), Tl()]
            t_xt, t_hT, t_hn, t_rs = Tl(), Tl(), Tl(), Tl()
            t_sq = [Tl(), Tl(), Tl(), Tl()]
            t_stg = [Tl(), Tl()]
            sk = 0
            for tt in range(NTT):
                t0 = tt * T
                S.dma("sp", lambda h, t0=t0: h.dma_start(out=xt[:], in_=xs[t0:t0 + T, :].rearrange("(tb p) d -> p tb d", p=128)), writes=[t_xt])
                S.epoch(t_hT)
                transpose_in(xt, t_xt, hT, t_hT, NCH, ident)
                S.dma("pool", lambda h, t0=t0: h.dma_start(out=hscr.rearrange("(c p) t -> p c t", p=128)[:, :, t0:t0 + T], in_=hT[:]), reads=[t_hT])
                rmsnorm(hT, t_hT, hn, t_hn, 0, sq, t_sq, rs, rstd, t_rs, tn1, t_tn1)
                for nb in range(8):
                    c0 = nb * 512 if nb < 4 else 3072 + (nb - 4) * 512
                    wv, t_w = WS.load([(wb["ab_in"][:, c0:c0 + 512], NCH, 512)])
                    si = sk % 2
                    sk += 1
                    S.epoch(t_stg[si])
                    for j in range(4):
                        b = bank()
                        fns = [lambda h, c=c, j=j, b=b, wv=wv: h.matmul(psb[b][:], wv[:, c, j * 128:(j + 1) * 128], hn[:, c, :], start=(c == 0), stop=(c == NCH - 1)) for c in range(NCH)]
                        S.grp("pe", fns, reads=[t_w, t_hn], writes=[t_ps[b]])
                        copy_op(evac_eng(), stg[si][:, j, :], psb[b][:], [t_ps[b]], [t_stg[si]], add=True)
                    S.dma("pool", lambda h, nb=nb, si=si, t0=t0: h.dma_start(out=qkscr[nb * 512:(nb + 1) * 512, t0:t0 + T].rearrange("(j p) t -> p j t", p=128), in_=stg[si][:]),
                          reads=[t_stg[si]])
                for nb in range(4):
                    c0 = 2048 + nb * 512 if nb < 2 else 5120 + (nb - 2) * 512
                    wv, t_w = WS.load([(wb["ab_in"][:, c0:c0 + 512], NCH, 512)])
                    si = sk % 2
                    sk += 1
                    S.epoch(t_stg[si])
                    for tb in range(4):
                        b = bank()
                        fns = [lambda h, c=c, tb=tb, b=b, wv=wv: h.matmul(psb[b][:], hn[:, c, tb * 128:(tb + 1) * 128], wv[:, c, :], start=(c == 0), stop=(c == NCH - 1)) for c in range(NCH)]
                        S.grp("pe", fns, reads=[t_w, t_hn], writes=[t_ps[b]])
                        copy_op(evac_eng(), stg[si][:, tb, :], psb[b][:], [t_ps[b]], [t_stg[si]], add=True)
                    S.dma("pool", lambda h, nb=nb, si=si, t0=t0: h.dma_start(out=vscr_a[t0:t0 + T, nb * 512:(nb + 1) * 512].rearrange("(tb p) n -> p tb n", p=128), in_=stg[si][:]),
                          reads=[t_stg[si]])
            S.barrier()

        if "stop1" in dbg:
            S.emit()
            return nc
        with ExitStack() as ph:
            LM = max(SEQS)
            NB = 2
            qT = [sb(ph, "qT%d" % i, [128, LM], BF16) for i in range(NB)]
            kTa = [sb(ph, "kTa%d" % i, [128, LM], BF16) for i in range(NB)]
            kTb = [sb(ph, "kTb%d" % i, [128, LM], BF16) for i in range(NB)]
            kstate = [None] * NB
            vv = [sb(ph, "vv%d" % i, [128, LM // 128, 128], BF16) for i in range(NB)]
            t_in = [Tl() for _ in range(NB)]
            NBI = 3
            bia = [sb(ph, "bia%d" % i, [128, 8, 512], F32) for i in range(NBI)]
            t_bia = [Tl() for _ in range(NBI)]
            far = sb(ph, "far", [128, 8, 2], F32)
            pt = [sb(ph, "pt%d" % i, [128, 512], BF16) for i in range(4)]
            t_pt = [Tl() for _ in range(4)]
            tmp = [sb(ph, "tmp%d" % i, [128, 512], F32) for i in range(4)]
            t_tmp = [Tl() for _ in range(4)]
            NE = 2
            eo = [[sb(ph, "eo%d_%d" % (e, i), [128, 512], F32) for i in range(6)] for e in range(NE)]
            esq = [sb(ph, "esq%d" % e, [128, 512], BF16) for e in range(NE)]
            est = [sb(ph, "est%d" % e, [128, 512], BF16) for e in range(NE)]
            t_e = [Tl() for _ in range(NE)]
            S.dma("sp", lambda h: h.dma_start(out=far[:], in_=t5f_d.rearrange("h p c -> p h c")), writes=[t_const], add=True)
            S.barrier()
            L0_NAMES = ["ab_out", "ffn_in0", "ffn_out0", "gate0", "proj0", "c_in"]
            cast_list = cast_thunks(L0_NAMES, 128)
            cast_late.extend(cast_thunks([k for k in W2D if k != "ab_in" and k not in L0_NAMES], 128))

            heads = []
            for si_, L in enumerate(SEQS):
                for hh in range(16):
                    heads.append((si_, L, seq_off[si_], hh))
            segs = []
            for hi, (si_, L, s0, hh) in enumerate(heads):
                if hh < 8:
                    segs.append((hi, "A", 0))
                else:
                    for ty in range(3):
                        segs.append((hi, "N", ty))
            seg_of = {(hi, ty): k for k, (hi, kind, ty) in enumerate(segs)}

            def load_head(hi):
                si_, L, s0, hh = heads[hi]
                ib = hi % NB
                isA = hh < 8
                h8 = hh % 8
                nkb = L // 128
                qrow = (0 if isA else 2048) + h8 * 128
                krow = qrow + 1024
                vcol = (0 if isA else 1024) + h8 * 128
                S.epoch(t_in[ib])
                if isA and kstate[ib] != "A":
                    S.op("dve", lambda h: h.memset(kTa[ib][64:128, :], 0.0), writes=[t_in[ib]], add=True)
                    S.op("dve", lambda h: h.memset(kTb[ib][0:64, :], 0.0), writes=[t_in[ib]], add=True)
                kstate[ib] = "A" if isA else "N"
                S.dma("sp", lambda h: h.dma_start(out=qT[ib][:, 0:L], in_=qkscr[qrow:qrow + 128, s0:s0 + L]), writes=[t_in[ib]], add=True)
                if isA:
                    S.dma("sp", lambda h: h.dma_start(out=kTa[ib][0:64, 0:L], in_=qkscr[krow:krow + 64, s0:s0 + L]), writes=[t_in[ib]], add=True)
                    S.dma("sp", lambda h: h.dma_start(out=kTb[ib][64:128, 0:L], in_=qkscr[krow + 64:krow + 128, s0:s0 + L]), writes=[t_in[ib]], add=True)
                else:
                    S.dma("sp", lambda h: h.dma_start(out=kTa[ib][:, 0:L], in_=qkscr[krow:krow + 128, s0:s0 + L]), writes=[t_in[ib]], add=True)
                for k8 in range(nkb // 8):
                    S.dma("sp", lambda h, k8=k8: h.dma_start(out=vv[ib][:, k8 * 8:(k8 + 1) * 8, :], in_=vscr_a[s0 + k8 * 1024:s0 + (k8 + 1) * 1024, vcol:vcol + 128].rearrange("(kb p) n -> p kb n", p=128)),
                          writes=[t_in[ib]], add=True)

            def load_seg(k):
                hi, kind, ty = segs[k]
                h8 = heads[hi][3] % 8
                bi = k % NBI
                if kind == "A":
                    S.dma("sp", lambda h: h.dma_start(out=bia[bi][:, 0:6, :], in_=t5t_d[h8]), writes=[t_bia[bi]])
                else:
                    S.dma("sp", lambda h: h.dma_start(out=bia[bi][:], in_=nat_d[ty, h8]), writes=[t_bia[bi]])

            U = []
            for hi, (si_, L, s0, hh) in enumerate(heads):
                isA = hh < 8
                nkb = L // 128
                nq = L // 512
                for Q in range(nq):
                    if isA:
                        kbs = list(range(nkb))
                        ty = 0
                    else:
                        rows = L // 64
                        ws = int(np.clip(8 * Q - 4, 0, rows - 16))
                        kbs = [ws // 2 + j for j in range(8)]
                        ty = 0 if Q == 0 else (2 if Q == nq - 1 else 1)
                    for ki, kb in enumerate(kbs):
                        for m in range(2 if isA else 1):
                            U.append(dict(hi=hi, Q=Q, ki=ki, kb=kb, m=m, ty=ty, first=(ki == 0), last=(ki == len(kbs) - 1),
                                          lastQ=(ki == len(kbs) - 1 and m == (1 if isA else 0)),
                                          newhead=(Q == 0 and ki == 0 and m == 0), newseg=(ki == 0 and m == 0)))
            cnt = {"e": 0}
            pend = []

            def stageA(i, u):
                si_, L, s0, hh = heads[u["hi"]]
                isA = hh < 8
                ib = u["hi"] % NB
                b = 4 + i % 4
                kb, Q, m = u["kb"], u["Q"], u["m"]
                kk_ = kTb[ib] if (isA and m == 1) else kTa[ib]
                S.op("pe", lambda h: h.matmul(psb[b][:], kk_[:, kb * 128:(kb + 1) * 128], qT[ib][:, Q * 512:(Q + 1) * 512], start=True, stop=True),
                     reads=[t_in[ib]], writes=[t_ps[b]])

            def stageB(i, u):
                si_, L, s0, hh = heads[u["hi"]]
                isA = hh < 8
                h8 = hh % 8
                b = 4 + i % 4
                ip = i % 4
                scale = 0.125 if isA else 128 ** -0.5
                jj = (u["kb"] - 4 * u["Q"] + 1) if isA else u["ki"]
                bi = seg_of[(u["hi"], u["ty"])] % NBI
                if (not isA) or (0 <= jj <= 5):
                    S.op("dve", lambda h: h.scalar_tensor_tensor(out=tmp[ip][:], in0=psb[b][:], scalar=scale, in1=bia[bi][:, jj, :], op0=ALU.mult, op1=ALU.add),
                         reads=[t_ps[b], t_bia[bi]], writes=[t_tmp[ip]])
                    S.op("act", lambda h: h.activation(out=pt[ip][:], in_=tmp[ip][:], func=AF.Exp), reads=[t_tmp[ip]], writes=[t_pt[ip]])
                else:
                    fc = 0 if jj < 0 else 1
                    S.op("act", lambda h: h.activation(out=pt[ip][:], in_=psb[b][:], func=AF.Exp, bias=far[:, h8, fc:fc + 1], scale=scale),
                         reads=[t_ps[b], t_const], writes=[t_pt[ip]])

            def stageC(i, u):
                ib = u["hi"] % NB
                ip = i % 4
                kb, m = u["kb"], u["m"]
                first, last = u["first"], u["last"]
                bo, bl = 2 * m, 2 * m + 1
                S.op("pe", lambda h: h.matmul(psb[bo][:], vv[ib][:, kb, :], pt[ip][:], start=first, stop=last), reads=[t_in[ib], t_pt[ip]], writes=[t_ps[bo]])
                S.op("pe", lambda h: h.matmul(psb[bl][:], ones[:], pt[ip][:], start=first, stop=last), reads=[t_const, t_pt[ip]], writes=[t_ps[bl]])

            def epilogue(i, u):
                si_, L, s0, hh = heads[u["hi"]]
                isA = hh < 8
                Q = u["Q"]
                e = cnt["e"] % NE
                cnt["e"] += 1
                o = eo[e]
                te = t_e[e]
                S.epoch(te)
                S.op("dve", lambda h: h.tensor_copy(out=o[1][:], in_=psb[1][:]), reads=[t_ps[1]], writes=[te], add=True)
                S.op("act", lambda h: h.copy(out=o[0][:], in_=psb[0][:]), reads=[t_ps[0]], writes=[te], add=True)
                if isA:
                    S.op("dve", lambda h: h.tensor_copy(out=o[3][:], in_=psb[3][:]), reads=[t_ps[3]], writes=[te], add=True)
                    S.op("act", lambda h: h.copy(out=o[2][:], in_=psb[2][:]), reads=[t_ps[2]], writes=[te], add=True)

                def store():
                    S.dma("pool", lambda h: h.dma_start(out=oT[hh * 128:(hh + 1) * 128, s0 + Q * 512:s0 + (Q + 1) * 512], in_=est[e][:]), reads=[te])
                    if si_ == len(SEQS) - 1:
                        for _ in range(4):
                            if cast_list:
                                cast_list.pop(0)()

                if isA:
                    def e2a():
                        S.op("act", lambda h: h.activation(out=o[1][:], in_=o[1][:], func=AF.Ln), reads=[te], writes=[te])
                        S.op("act", lambda h: h.activation(out=o[1][:], in_=o[1][:], func=AF.Exp, scale=-1.0), reads=[te], writes=[te])
                        S.op("act", lambda h: h.activation(out=o[3][:], in_=o[3][:], func=AF.Ln), reads=[te], writes=[te])
                        S.op("act", lambda h: h.activation(out=o[3][:], in_=o[3][:], func=AF.Exp, scale=-1.0), reads=[te], writes=[te])
                        S.op("dve", lambda h: h.tensor_tensor(out=o[0][:], in0=o[0][:], in1=o[1][:], op=ALU.mult), reads=[te], writes=[te])
                        S.op("dve", lambda h: h.tensor_tensor(out=o[2][:], in0=o[2][:], in1=o[3][:], op=ALU.mult), reads=[te], writes=[te])
                        S.op("dve", lambda h: h.scalar_tensor_tensor(out=o[4][:], in0=o[2][:], scalar=lamt[:, 1:2], in1=o[0][:], op0=ALU.mult, op1=ALU.add), reads=[te, t_const], writes=[te])
                        S.op("dve", lambda h: h.tensor_tensor(out=esq[e][:], in0=o[4][:], in1=o[4][:], op=ALU.mult), reads=[te], writes=[te])
                        b = 4 + (i + 1) % 4
                        S.op("pe", lambda h: h.matmul(psb[b][:], ones[:], esq[e][:], start=True, stop=True), reads=[te, t_const], writes=[t_ps[b]])
                        S.op("act", lambda h: h.activation(out=o[5][:], in_=psb[b][:], func=AF.Ln, bias=cst[:, 0:1], scale=1.0 / 128), reads=[t_ps[b], t_const, te], writes=[te])

                    def e2b():
                        S.op("act", lambda h: h.activation(out=o[1][:], in_=o[5][:], func=AF.Exp, scale=-0.5), reads=[te], writes=[te])
                        S.op("dve", lambda h: h.scalar_tensor_tensor(out=o[2][:], in0=o[4][:], scalar=subg[:, 0:1], in1=o[1][:], op0=ALU.mult, op1=ALU.mult), reads=[te, t_const], writes=[te])
                        S.op("dve", lambda h: h.tensor_scalar(out=est[e][:], in0=o[2][:], scalar1=0.8, scalar2=None, op0=ALU.mult), reads=[te], writes=[te])
                        store()
                    pend.append((i + 6, e2a))
                    pend.append((i + 12, e2b))
                else:
                    def e2():
                        S.op("act", lambda h: h.activation(out=o[1][:], in_=o[1][:], func=AF.Ln), reads=[te], writes=[te])
                        S.op("act", lambda h: h.activation(out=o[1][:], in_=o[1][:], func=AF.Exp, scale=-1.0), reads=[te], writes=[te])
                        S.op("dve", lambda h: h.tensor_tensor(out=est[e][:], in0=o[0][:], in1=o[1][:], op=ALU.mult), reads=[te], writes=[te])
                        store()
                    pend.append((i + 3, e2))

            LA = 3
            load_head(0)
            load_seg(0)
            nU = len(U)
            for i in range(nU + LA):
                if i < nU:
                    u = U[i]
                    if u["newhead"] and u["hi"] + 1 < len(heads):
                        pend.append((i + LA + 1, (lambda hn_=u["hi"] + 1: load_head(hn_))))
                    if u["newseg"]:
                        k = seg_of[(u["hi"], u["ty"])]
                        first_use = (u["ty"] == 0 and u["Q"] == 0) or (u["ty"] == 1 and u["Q"] == 1) or (u["ty"] == 2)
                        if first_use and k + 1 < len(segs):
                            load_seg(k + 1)
                    stageA(i, u)
                    stageB(i, u)
                if i >= LA:
                    uc = U[i - LA]
                    stageC(i - LA, uc)
                    if uc["lastQ"]:
                        epilogue(i - LA, uc)
                while pend and pend[0][0] <= i:
                    pend.pop(0)[1]()
                pend.sort(key=lambda x: x[0])
            while pend:
                pend.pop(0)[1]()
            while cast_list:
                cast_list.pop(0)()
            S.barrier()

        if "stop2" in dbg:
            S.emit()
            return nc

        def chain(li):
            with ExitStack() as ph:
                WS = WStream(ph)
                hT = sb(ph, "c_hT", [128, NCH, T], F32)
                hn = sb(ph, "c_hn", [128, NCH, T], BF16)
                oin = hn
                act = sb(ph, "c_act", [128, 44, T], BF16)
                sq = [sb(ph, "c_sq%d" % i, [128, T], BF16) for i in range(4)]
                rs = sb(ph, "c_rs", [128, T], F32)
                rstd = sb(ph, "c_rstd", [128, T], F32)
                tm = [sb(ph, "c_tm%d" % i, [128, T], F32) for i in range(4)]
                t_tm = [Tl() for _ in range(4)]
                ptm = sb(ph, "c_ptm", [128, 4, 256], F32)
                pT = sb(ph, "c_pT", [128, 2, T], BF16)
                wproj = sb(ph, "c_wproj", [128, 2, D], BF16)
                t_hT, t_hn, t_oin, t_act, t_rs, t_ptm, t_pT, t_wp = [Tl() for _ in range(8)]
                t_oin = t_hn
                t_sq = [Tl(), Tl(), Tl(), Tl()]
                ostg = sb(ph, "c_ostg", [128, 4, 512], F32)
                t_ostg = Tl()
                kt = 0
                S.dma("sp", lambda h: h.dma_start(out=wproj[:], in_=wb["proj%d" % li].rearrange("(c p) n -> p c n", p=128)), writes=[t_wp])
                wmix = wb["ab_out"] if li == 0 else wb["c_out"]
                for tt in range(NTT):
                    t0 = tt * T
                    sidx = max(i for i in range(len(SEQS)) if seq_off[i] <= t0)
                    S.dma("sp", lambda h, t0=t0: h.dma_start(out=hT[:], in_=hscr.rearrange("(c p) t -> p c t", p=128)[:, :, t0:t0 + T]), writes=[t_hT])
                    S.dma("sp", lambda h, t0=t0: h.dma_start(out=oin[:], in_=oT.rearrange("(c p) t -> p c t", p=128)[:, :, t0:t0 + T]), writes=[t_oin])
                    S.dma("sp", lambda h, t0=t0: h.dma_start(out=ptm[:], in_=pp[li, t0:t0 + T, :].rearrange("(tb p) d -> p tb d", p=128)), writes=[t_ptm])
                    for nb in range(4):
                        wv, t_w = WS.load([(wmix[:, nb * 512:(nb + 1) * 512], NCH, 512)])
                        for j in range(4):
                            b = bank()
                            cc = nb * 4 + j
                            fns = [lambda h, c=c, j=j, b=b, wv=wv: h.matmul(psb[b][:], wv[:, c, j * 128:(j + 1) * 128], oin[:, c, :], start=(c == 0), stop=(c == NCH - 1)) for c in range(NCH)]
                            S.grp("pe", fns, reads=[t_w, t_oin], writes=[t_ps[b]])
                            S.op("dve", lambda h, cc=cc, b=b: h.tensor_tensor(out=hT[:, cc, :], in0=hT[:, cc, :], in1=psb[b][:], op=ALU.add), reads=[t_ps[b], t_hT], writes=[t_hT])
                    rmsnorm_pre(hT, t_hT, hn, t_hn, 2 + li)
                    wfi = wb["ffn_in%d" % li]
                    S.epoch(t_act)
                    epis = []
                    for u in range(22):
                        wv, t_w = WS.load([(wfi[:, u * 256:(u + 1) * 256], NCH, 256), (wfi[:, DFF + u * 256:DFF + (u + 1) * 256], NCH, 256)])
                        for j in range(2):
                            bg = bank()
                            bu = bank()
                            fns = [lambda h, c=c, j=j, b=bg, wv=wv: h.matmul(psb[b][:], wv[:, c, j * 128:(j + 1) * 128], hn[:, c, :], start=(c == 0), stop=(c == NCH - 1)) for c in range(NCH)]
                            S.grp("pe", fns, reads=[t_w, t_hn], writes=[t_ps[bg]])
                            fns = [lambda h, c=c, j=j, b=bu, wv=wv: h.matmul(psb[b][:], wv[:, c, 256 + j * 128:256 + (j + 1) * 128], hn[:, c, :], start=(c == 0), stop=(c == NCH - 1)) for c in range(NCH)]
                            S.grp("pe", fns, reads=[t_w, t_hn], writes=[t_ps[bu]])
                            it = kt % 4
                            kt += 1

                            def epi(bg=bg, bu=bu, it=it, fc=u * 2 + j):
                                S.op("dve", lambda h: h.tensor_tensor(out=tm[it][:], in0=psb[bg][:], in1=rstd[:], op=ALU.mult), reads=[t_ps[bg], t_rs], writes=[t_tm[it]])
                                S.op("act", lambda h: h.activation(out=tm[it][:], in_=tm[it][:], func=AF.Silu), reads=[t_tm[it]], writes=[t_tm[it]])
                                S.op("pool", lambda h: h.tensor_tensor(out=tm[it][:], in0=tm[it][:], in1=rstd[:], op=ALU.mult), reads=[t_tm[it], t_rs], writes=[t_tm[it]])
                                S.op("dve", lambda h: h.tensor_tensor(out=act[:, fc, :], in0=tm[it][:], in1=psb[bu][:], op=ALU.mult),
                                     reads=[t_tm[it], t_ps[bu]], writes=[t_act], add=True)
                            if u == 0:
                                epis.append(epi)
                                if j == 1:
                                    rmsnorm_stats(hT, t_hT, sq, t_sq, rs, rstd, t_rs)
                                    for e_ in epis:
                                        e_()
                            else:
                                epi()
                    wfo = wb["ffn_out%d" % li]
                    for nb in range(8):
                        b2 = [bank(), bank()]
                        for kh in range(2):
                            wv, t_w = WS.load([(wfo[kh * 22 * 128:(kh + 1) * 22 * 128, nb * 256:(nb + 1) * 256], 22, 256)])
                            for j in range(2):
                                b = b2[j]
                                fns = [lambda h, c=c, j=j, b=b, wv=wv, kh=kh: h.matmul(psb[b][:], wv[:, c, j * 128:(j + 1) * 128], act[:, kh * 22 + c, :], start=(kh == 0 and c == 0), stop=(kh == 1 and c == 21)) for c in range(22)]
                                S.grp("pe", fns, reads=[t_w, t_act], writes=[t_ps[b]])
                        for j in range(2):
                            cc = nb * 2 + j
                            S.op("dve", lambda h, cc=cc, b=b2[j]: h.tensor_tensor(out=hT[:, cc, :], in0=hT[:, cc, :], in1=psb[b][:], op=ALU.add), reads=[t_ps[b2[j]], t_hT], writes=[t_hT])
                    rmsnorm_pre(hT, t_hT, hn, t_hn, 4 + li)
                    S.epoch(t_pT)
                    transpose_in(ptm, t_ptm, pT, t_pT, 2, ident)
                    wg = wb["gate%d" % li]
                    epis = []
                    for nb in range(4):
                        wv, t_w = WS.load([(wg[:, nb * 512:(nb + 1) * 512], NCH, 512)])
                        for j in range(4):
                            cc = nb * 4 + j
                            bg = bank()
                            bp = bank()
                            fns = [lambda h, c=c, j=j, b=bg, wv=wv: h.matmul(psb[b][:], wv[:, c, j * 128:(j + 1) * 128], hn[:, c, :], start=(c == 0), stop=(c == NCH - 1)) for c in range(NCH)]
                            S.grp("pe", fns, reads=[t_w, t_hn], writes=[t_ps[bg]])
                            fns = [lambda h, c=c, cc=cc, b=bp: h.matmul(psb[b][:], wproj[:, c, cc * 128:(cc + 1) * 128], pT[:, c, :], start=(c == 0), stop=(c == 1)) for c in range(2)]
                            S.grp("pe", fns, reads=[t_wp, t_pT], writes=[t_ps[bp]])
                            it = kt % 4
                            kt += 1

                            def epi(bg=bg, bp=bp, it=it, cc=cc):
                                S.op("dve", lambda h: h.tensor_tensor(out=tm[it][:], in0=psb[bg][:], in1=rstd[:], op=ALU.mult), reads=[t_ps[bg], t_rs], writes=[t_tm[it]])
                                S.op("act", lambda h: h.activation(out=tm[it][:], in_=tm[it][:], func=AF.Sigmoid), reads=[t_tm[it]], writes=[t_tm[it]])
                                S.op("dve", lambda h: h.tensor_tensor(out=tm[it][:], in0=tm[it][:], in1=psb[bp][:], op=ALU.mult), reads=[t_tm[it], t_ps[bp]], writes=[t_tm[it]])
                                S.op("dve", lambda h: h.tensor_tensor(out=hT[:, cc, :], in0=hT[:, cc, :], in1=tm[it][:], op=ALU.add), reads=[t_tm[it], t_hT], writes=[t_hT])
                            if nb == 0 and j < 2:
                                epis.append(epi)
                                if j == 1:
                                    rmsnorm_stats(hT, t_hT, sq, t_sq, rs, rstd, t_rs)
                                    for e_ in epis:
                                        e_()
                            else:
                                epi()
                    if li == 0:
                        S.dma("pool", lambda h, t0=t0: h.dma_start(out=hscr.rearrange("(c p) t -> p c t", p=128)[:, :, t0:t0 + T], in_=hT[:]), reads=[t_hT])
                        rmsnorm_pre(hT, t_hT, hn, t_hn, 1)
                        zc0 = zoff[sidx] + 1 + (t0 - seq_off[sidx])
                        epis = []
                        for nb in range(12):
                            wv, t_w = WS.load([(wb["c_in"][:, nb * 512:(nb + 1) * 512], NCH, 512)])
                            for j in range(4):
                                b = bank()
                                fns = [lambda h, c=c, j=j, b=b, wv=wv: h.matmul(psb[b][:], wv[:, c, j * 128:(j + 1) * 128], hn[:, c, :], start=(c == 0), stop=(c == NCH - 1)) for c in range(NCH)]
                                S.grp("pe", fns, reads=[t_w, t_hn], writes=[t_ps[b]])
                                it = kt % 4
                                kt += 1
                                r0 = (nb * 4 + j) * 128

                                def epi(b=b, it=it, r0=r0, zc0=zc0):
                                    S.op("dve", lambda h: h.tensor_tensor(out=tm[it][:], in0=psb[b][:], in1=rstd[:], op=ALU.mult), reads=[t_ps[b], t_rs], writes=[t_tm[it]])
                                    S.dma("pool", lambda h: h.dma_start(out=zscr[r0:r0 + 128, zc0:zc0 + T], in_=tm[it][:]), reads=[t_tm[it]])
                                if nb == 0:
                                    epis.append(epi)
                                    if j == 3:
                                        rmsnorm_stats(hT, t_hT, sq, t_sq, rs, rstd, t_rs)
                                        for e_ in epis:
                                            e_()
                                else:
                                    epi()
                    else:
                        rmsnorm(hT, t_hT, None, None, 6, sq, t_sq, rs, rstd, t_rs)
                        for dq in range(4):
                            S.epoch(t_ostg)
                            for j in range(4):
                                cc = dq * 4 + j
                                it = kt % 4
                                kt += 1
                                S.op("dve", lambda h, it=it, cc=cc: h.scalar_tensor_tensor(out=tm[it][:], in0=hT[:, cc, :], scalar=gains[:, 6, cc:cc + 1], in1=rstd[:], op0=ALU.mult, op1=ALU.mult),
                                     reads=[t_hT, t_rs, t_const], writes=[t_tm[it]])
                                b = bank()
                                fns = [lambda h, tb=tb, b=b, it=it: h.transpose(psb[b][:, tb * 128:(tb + 1) * 128], tm[it][:, tb * 128:(tb + 1) * 128], ident[:]) for tb in range(4)]
                                S.grp("pe", fns, reads=[t_tm[it], t_const], writes=[t_ps[b]])
                                copy_op(evac_eng(), ostg[:, :, j * 128:(j + 1) * 128], psb[b][:].rearrange("p (tb n) -> p tb n", n=128), [t_ps[b]], [t_ostg], add=True)
                            S.dma("pool", lambda h, dq=dq, t0=t0: h.dma_start(out=y_out[t0:t0 + T, dq * 512:(dq + 1) * 512].rearrange("(tb p) n -> p tb n", p=128), in_=ostg[:]), reads=[t_ostg])
                S.barrier()

        chain(0)
        if "stop3" in dbg:
            S.emit()
            return nc

        def sin_reduce(ph_tiles, arg, out, npart, t_arg, t_out, split=None):
            u, ni, nf_, neg = ph_tiles
            P = slice(0, npart)
            S.op("dve", lambda h: h.tensor_scalar(out=u[P, :], in0=arg[P, :], scalar1=1.0 / (2 * math.pi), scalar2=8.5, op0=ALU.mult, op1=ALU.add), reads=[t_arg], writes=[t_arg])
            S.op("dve", lambda h: h.tensor_copy(out=ni[P, :], in_=u[P, :]), reads=[t_arg], writes=[t_arg])
            S.op("dve", lambda h: h.tensor_copy(out=nf_[P, :], in_=ni[P, :]), reads=[t_arg], writes=[t_arg])
            S.op("dve", lambda h: h.tensor_tensor(out=u[P, :], in0=u[P, :], in1=nf_[P, :], op=ALU.subtract), reads=[t_arg], writes=[t_arg])
            S.op("dve", lambda h: h.tensor_scalar(out=neg[P, :], in0=u[P, :], scalar1=0.0, scalar2=None, op0=ALU.is_lt), reads=[t_arg], writes=[t_arg])
            S.op("dve", lambda h: h.tensor_tensor(out=u[P, :], in0=u[P, :], in1=neg[P, :], op=ALU.add), reads=[t_arg], writes=[t_arg])
            if split is None:
                S.op("act", lambda h: h.activation(out=out, in_=u[P, :], func=AF.Sin, bias=cst[P, 1:2], scale=6.28318), reads=[t_arg, t_const], writes=[t_out], add=True)
            else:
                dst_, tq_, Lh_ = split
                S.op("act", lambda h: h.activation(out=dst_[:, tq_ * 256:(tq_ + 1) * 256], in_=u[P, 0:512:2], func=AF.Sin, bias=cst[P, 1:2], scale=6.28318), reads=[t_arg, t_const], writes=[t_out], add=True)
                S.op("act", lambda h: h.activation(out=dst_[:, Lh_ + tq_ * 256:Lh_ + (tq_ + 1) * 256], in_=u[P, 1:512:2], func=AF.Sin, bias=cst[P, 1:2], scale=6.28318), reads=[t_arg, t_const], writes=[t_out], add=True)

        with ExitStack() as ph0:
            cw = sb(ph0, "cw", [128, 48, 4], F32)
            skp = sb(ph0, "skp", [128, NCH], F32)
            zcol = sb(ph0, "zcol", [128, 48, 1], F32)
            altc = sb(ph0, "altc", [128, 2], BF16)
            altr = sb(ph0, "altr", [1, 512], BF16)
            altf = sb(ph0, "altf", [128, 512], F32)
            t_h0 = Tl()
            for k in range(3):
                for c3 in range(3):
                    S.dma("sp", lambda h, k=k, c3=c3: h.dma_start(out=cw[:, c3 * 16:(c3 + 1) * 16, k], in_=c_conv_w[0, k, c3 * 2048:(c3 + 1) * 2048].rearrange("(c p) -> p c", p=128), allow_slow_non_contiguous=True), writes=[t_h0], add=True)
            for c3 in range(3):
                S.dma("sp", lambda h, c3=c3: h.dma_start(out=cw[:, c3 * 16:(c3 + 1) * 16, 3], in_=c_conv_b[0, c3 * 2048:(c3 + 1) * 2048].rearrange("(c p) -> p c", p=128), allow_slow_non_contiguous=True), writes=[t_h0], add=True)
            S.dma("sp", lambda h: h.dma_start(out=skp[:], in_=c_skip[0].rearrange("(c p) -> p c", p=128), allow_slow_non_contiguous=True), writes=[t_h0], add=True)
            S.op("pool", lambda h: h.memset(zcol[:], 0.0), writes=[t_h0], add=True)
            S.dma("sp", lambda h: h.dma_start(out=altf[:], in_=alt_d[:, :]), writes=[t_h0], add=True)
            S.barrier()
            S.op("dve", lambda h: h.tensor_copy(out=altc[:, 0:1], in_=altf[:, 0:1]), reads=[t_h0], writes=[t_h0], add=True)
            S.op("dve", lambda h: h.tensor_copy(out=altr[:], in_=altf[0:1, :]), reads=[t_h0], writes=[t_h0], add=True)
            S.barrier()
            for i, L in enumerate(SEQS):
                for col in (zoff[i], zoff[i] + L + 1):
                    for c3 in range(3):
                        S.dma("pool", lambda h, col=col, c3=c3: h.dma_start(out=zscr[c3 * 2048:(c3 + 1) * 2048, col:col + 1].rearrange("(c p) t -> p c t", p=128), in_=zcol[:, 0:16, :], allow_slow_non_contiguous=True), reads=[t_h0])
            S.barrier()

            def hyena_seq(si_, L):
                s0 = seq_off[si_]
                nfb = L // 128
                nh = nfb // 2
                Lh = L // 2
                ntq = L // 512
                N = 2 * L
                Hs = Hscr[si_]
                CE, SE, CO, SO, COT, SOT = [dftm[L][k] for k in range(6)]
                def load_fwd(cfv, sfv, tl, fb):
                    for mi, (dst, mat) in enumerate(((cfv[:, 0:nh, :], CE), (cfv[:, nh:nfb, :], CO), (sfv[:, 0:nh, :], SE), (sfv[:, nh:nfb, :], SO))):
                        for k8 in range(max(1, nh // 8)):
                            w8 = min(8, nh)
                            S.dma("sp", lambda h, dst=dst, mat=mat, k8=k8, w8=w8: h.dma_start(out=dst[:, k8 * w8:(k8 + 1) * w8, :], in_=mat[k8 * w8 * 128:(k8 + 1) * w8 * 128, fb * 128:(fb + 1) * 128].rearrange("(tb p) f -> p tb f", p=128)),
                                  writes=[tl], add=not (mi == 0 and k8 == 0))

                with ExitStack() as ph:
                    hd2 = sb(ph, "hd2", [64, L], F32)
                    w3 = sb(ph, "fw3", [64, 2 * D], F32)
                    negt = sb(ph, "negt", [128, nfb], F32)
                    wn = sb(ph, "wn", [128, nh], F32)
                    delt = sb(ph, "delt", [128, D], F32)
                    t_f, t_arg, t_hd1, t_hd2, t_hs = Tl(), Tl(), Tl(), Tl(), Tl()
                    S.dma("sp", lambda h: h.dma_start(out=w3[:], in_=f_w3[0]), writes=[t_f], add=True)
                    S.dma("sp", lambda h: h.dma_start(out=negt[:], in_=negt_d[L][:, :]), writes=[t_f], add=True)
                    S.dma("sp", lambda h: h.dma_start(out=wn[:], in_=wn_d[L][:, :]), writes=[t_f], add=True)
                    S.dma("sp", lambda h: h.dma_start(out=delt[:], in_=delta_d[:, :]), writes=[t_f], add=True)
                    with ExitStack() as ph1:
                        featT = sb(ph1, "featT", [33, L], F32)
                        w1 = sb(ph1, "fw1", [33, 64], F32)
                        w2 = sb(ph1, "fw2", [64, 64], F32)
                        fb_ = sb(ph1, "fbias", [64, 4], F32)
                        hd1 = sb(ph1, "hd1", [64, L], F32)
                        argt = sb(ph1, "argt", [64, 512], F32)
                        srt = (sb(ph1, "sr_u", [64, 512], F32), sb(ph1, "sr_i", [64, 512], I32), sb(ph1, "sr_f", [64, 512], F32), sb(ph1, "sr_n", [64, 512], F32))
                        S.dma("sp", lambda h: h.dma_start(out=featT[:], in_=feat_d[L][:, :]), writes=[t_f], add=True)
                        S.dma("sp", lambda h: h.dma_start(out=w1[:], in_=f_w1[0]), writes=[t_f], add=True)
                        S.dma("sp", lambda h: h.dma_start(out=w2[:], in_=f_w2[0]), writes=[t_f], add=True)
                        S.dma("sp", lambda h: h.dma_start(out=fb_[:, 0:1], in_=f_b1[0].rearrange("(p o) -> p o", o=1), allow_slow_non_contiguous=True), writes=[t_f], add=True)
                        S.dma("sp", lambda h: h.dma_start(out=fb_[:, 1:2], in_=f_b2[0].rearrange("(p o) -> p o", o=1), allow_slow_non_contiguous=True), writes=[t_f], add=True)
                        S.dma("sp", lambda h: h.dma_start(out=fb_[:, 2:3], in_=f_freq[0, 0].rearrange("(p o) -> p o", o=1), allow_slow_non_contiguous=True), writes=[t_f], add=True)
                        S.dma("sp", lambda h: h.dma_start(out=fb_[:, 3:4], in_=f_freq[0, 1].rearrange("(p o) -> p o", o=1), allow_slow_non_contiguous=True), writes=[t_f], add=True)
                        for (src_, wgt, kdim, bc, fc, dst, t_src, t_dst) in ((featT, w1, 33, 0, 2, hd1, t_f, t_hd1), (hd1, w2, 64, 1, 3, hd2, t_hd1, t_hd2)):
                            for tq in range(ntq):
                                b = bank()
                                S.op("pe", lambda h, b=b, src_=src_, wgt=wgt, kdim=kdim, tq=tq: h.matmul(psb[b][0:64, :], wgt[0:kdim, :], src_[0:kdim, tq * 512:(tq + 1) * 512], start=True, stop=True),
                                     reads=[t_f, t_src], writes=[t_ps[b]])
                                S.op("dve", lambda h, b=b, bc=bc, fc=fc: h.tensor_scalar(out=argt[:], in0=psb[b][0:64, :], scalar1=fb_[:, bc:bc + 1], scalar2=fb_[:, fc:fc + 1], op0=ALU.add, op1=ALU.mult),
                                     reads=[t_ps[b], t_f], writes=[t_arg])
                                sin_reduce(srt, argt, dst[:, tq * 512:(tq + 1) * 512], 64, t_arg, t_dst, split=(None if dst is hd1 else (dst, tq, Lh)))
                        S.barrier()
                    hs = sb(ph, "hs", [128, nfb, 512], BF16)
                    hdd = sb(ph, "hdd", [128, nfb, 512], BF16)
                    dec = [sb(ph, "dec%d" % i, [128, 512], F32) for i in range(2)]
                    hf = [sb(ph, "hf%d" % i, [128, 512], F32) for i in range(2)]
                    hb = [sb(ph, "hb%d" % i, [128, 512], F32) for i in range(2)]
                    cf = [sb(ph, "cf%d" % i, [128, nfb, 128], BF16) for i in range(2)]
                    sf = [sb(ph, "sf%d" % i, [128, nfb, 128], BF16) for i in range(2)]
                    hst = [sb(ph, "hst%d" % i, [128, 4, 512], F32) for i in range(2)]
                    hto = [sb(ph, "hto%d" % i, [128, 2, 512], F32) for i in range(2)]
                    t_dec = [Tl(), Tl()]
                    t_cf = [Tl(), Tl()]
                    t_hst = [Tl(), Tl()]
                    t_hto = [Tl(), Tl()]
                    kd = 0
                    kc = 0
                    ks = 0

                    def _cb_a(cb):
                        nonlocal kd, kc, ks
                        S.epoch(t_hs)
                        for rb in range(nfb):
                            i = kd % 2
                            kd += 1
                            bF = bank()
                            bB = bank()
                            S.op("pe", lambda h, b=bF, rb=rb: h.matmul(psb[b][:], hd2[:, rb * 128:(rb + 1) * 128], w3[:, cb * 512:(cb + 1) * 512], start=True, stop=True), reads=[t_hd2, t_f], writes=[t_ps[bF]])
                            S.op("pe", lambda h, b=bB, rb=rb: h.matmul(psb[b][:], hd2[:, rb * 128:(rb + 1) * 128], w3[:, D + cb * 512:D + (cb + 1) * 512], start=True, stop=True), reads=[t_hd2, t_f], writes=[t_ps[bB]])
                            S.op("act", lambda h, i=i, rb=rb: h.activation(out=dec[i][:], in_=delt[:, cb * 512:(cb + 1) * 512], func=AF.Exp, scale=negt[:, rb:rb + 1]), reads=[t_f], writes=[t_dec[i]])
                            S.op("dve", lambda h, i=i, b=bF: h.tensor_tensor(out=hf[i][:], in0=psb[b][:], in1=dec[i][:], op=ALU.mult), reads=[t_ps[bF], t_dec[i]], writes=[t_dec[i]])
                            S.op("dve", lambda h, i=i, b=bB: h.tensor_tensor(out=hb[i][:], in0=psb[b][:], in1=dec[i][:], op=ALU.mult), reads=[t_ps[bB], t_dec[i]], writes=[t_dec[i]])
                            S.op("dve", lambda h, i=i, rb=rb: h.tensor_tensor(out=hdd[:, rb, :], in0=hf[i][:], in1=hb[i][:], op=ALU.subtract), reads=[t_dec[i]], writes=[t_hs], add=True)
                            if rb == 0:
                                S.op("dve", lambda h, i=i: h.tensor_scalar(out=hb[i][:], in0=hb[i][:], scalar1=cst[:, 3:4], scalar2=None, op0=ALU.mult), reads=[t_dec[i], t_const], writes=[t_dec[i]])
                            S.op("dve", lambda h, i=i, rb=rb: h.tensor_tensor(out=hs[:, rb, :], in0=hf[i][:], in1=hb[i][:], op=ALU.add), reads=[t_dec[i]], writes=[t_hs], add=True)
                        for fb in range(nh + 1):
                            half = fb == nh
                            if not half:
                                i = kc % 2
                                kc += 1
                                load_fwd(cf[i], sf[i], t_cf[i], fb)
                                bks = [bank() for _ in range(4)]
                                for q, (mt, xs_, off) in enumerate(((cf[i], hs, 0), (cf[i], hs, nh), (sf[i], hdd, 0), (sf[i], hdd, nh))):
                                    fns = [lambda h, tb=tb, b=bks[q], mt=mt, xs_=xs_, off=off: h.matmul(psb[b][:], mt[:, off + tb, :], xs_[:, off + tb, :], start=(tb == 0), stop=(tb == nh - 1)) for tb in range(nh)]
                                    S.grp("pe", fns, reads=[t_cf[i], t_hs], writes=[t_ps[bks[q]]])
                                j = ks % 2
                                ks += 1
                                S.op("act", lambda h, j=j, b=bks[1], fb=fb: h.activation(out=hto[j][:, 0, :], in_=psb[b][:], func=AF.Copy, scale=wn[:, fb:fb + 1]), reads=[t_ps[bks[1]], t_f], writes=[t_hto[j]])
                                S.op("act", lambda h, j=j, b=bks[3], fb=fb: h.activation(out=hto[j][:, 1, :], in_=psb[b][:], func=AF.Copy, scale=wn[:, fb:fb + 1]), reads=[t_ps[bks[3]], t_f], writes=[t_hto[j]], add=True)
                                S.epoch(t_hst[j])
                                for slot, bsrc, osl, op1 in ((0, bks[0], 0, ALU.add), (2, bks[0], 0, ALU.subtract), (1, bks[2], 1, ALU.add), (3, bks[2], 1, ALU.subtract)):
                                    S.op("dve", lambda h, j=j, slot=slot, bsrc=bsrc, osl=osl, op1=op1, fb=fb: h.scalar_tensor_tensor(out=hst[j][:, slot, :], in0=psb[bsrc][:], scalar=wn[:, fb:fb + 1], in1=hto[j][:, osl, :], op0=ALU.mult, op1=op1),
                                         reads=[t_ps[bsrc], t_hto[j], t_f], writes=[t_hst[j]], add=True)
                                S.dma("pool", lambda h, j=j, fb=fb: h.dma_start(out=Hs[:, fb * 128:(fb + 1) * 128, cb * 512:(cb + 1) * 512].rearrange("a p n -> p a n"), in_=hst[j][:]), reads=[t_hst[j]])
                            else:
                                bP = bank()
                                bQ = bank()
                                fns = [lambda h, tb=tb, b=bP: h.matmul(psb[b][0:1, :], altc[:, 0:1], hs[:, tb, :], start=(tb == 0), stop=(tb == nh - 1)) for tb in range(nh)]
                                S.grp("pe", fns, reads=[t_h0, t_hs], writes=[t_ps[bP]])
                                fns = [lambda h, tb=tb, b=bQ: h.matmul(psb[b][0:1, :], altc[:, 0:1], hdd[:, nh + tb, :], start=(tb == 0), stop=(tb == nh - 1)) for tb in range(nh)]
                                S.grp("pe", fns, reads=[t_h0, t_hs], writes=[t_ps[bQ]])
                                j = ks % 2
                                ks += 1
                                S.op("act", lambda h, j=j, b=bP: h.activation(out=hst[j][0:1, 0, :], in_=psb[b][0:1, :], func=AF.Copy, scale=2.0 / N), reads=[t_ps[bP]], writes=[t_hst[j]])
                                S.op("act", lambda h, j=j, b=bQ: h.activation(out=hst[j][0:1, 1, :], in_=psb[b][0:1, :], func=AF.Copy, scale=2.0 / N), reads=[t_ps[bQ]], writes=[t_hst[j]], add=True)
                                S.dma("pool", lambda h, j=j: h.dma_start(out=Hs[0:2, Lh:Lh + 1, cb * 512:(cb + 1) * 512].rearrange("a p n -> p a n"), in_=hst[j][0:1, 0:2, :]), reads=[t_hst[j]])
                    for cb in range(4):
                        _cb_a(cb)
                    S.barrier()
                if "stop4a" in dbg:
                    return
                with ExitStack() as ph:
                    vTM = sb(ph, "vTM", [128, nfb, 512], BF16)
                    G = [sb(ph, "G%d" % i, [128, nh, 512], BF16) for i in range(4)]
                    YLh = sb(ph, "YLh", [1, 2, 512], BF16)
                    dft = [sb(ph, "dft%d" % i, [128, 2, 4096], BF16) for i in range(2)]
                    t_dft = [Tl(), Tl()]
                    zh = [sb(ph, "zh%d" % i, [128, 514], F32) for i in range(4)]
                    t_zh = [Tl() for _ in range(4)]
                    cv = [sb(ph, "cv%d" % i, [128, 512], F32) for i in range(4)]
                    t_cv = [Tl() for _ in range(4)]
                    vb = [sb(ph, "vb%d" % i, [128, 512], BF16) for i in range(2)]
                    t_vb = [Tl(), Tl()]
                    hPs = [sb(ph, "hP%d" % i, [128, 4, 512], F32) for i in range(2)]
                    t_hPs = [Tl(), Tl()]
                    khp = 0
                    wk = [sb(ph, "wk%d" % i, [128, 512], F32) for i in range(6)]
                    t_wkd, t_wkp, t_wk4, t_wk5 = Tl(), Tl(), Tl(), Tl()
                    yo = [sb(ph, "yo%d" % i, [128, 512], BF16) for i in range(2)]
                    t_yo = [Tl(), Tl()]
                    t_vTM, t_Y = Tl(), Tl()
                    kz = 0
                    kv = 0
                    kdf = 0
                    kyo = 0

                    def conv(zi, chunk, out, t_out_list):
                        S.op("act", lambda h: h.activation(out=out, in_=zh[zi][:, 0:512], func=AF.Identity, bias=cw[:, chunk, 3:4], scale=cw[:, chunk, 0:1]), reads=[t_zh[zi], t_h0], writes=t_out_list)
                        S.op("dve", lambda h: h.scalar_tensor_tensor(out=out, in0=zh[zi][:, 1:513], scalar=cw[:, chunk, 1:2], in1=out, op0=ALU.mult, op1=ALU.add), reads=[t_zh[zi], t_h0] + t_out_list, writes=t_out_list)
                        S.op("dve", lambda h: h.scalar_tensor_tensor(out=out, in0=zh[zi][:, 2:514], scalar=cw[:, chunk, 2:3], in1=out, op0=ALU.mult, op1=ALU.add), reads=[t_zh[zi], t_h0] + t_out_list, writes=t_out_list)

                    def tt(eng, out, a, b_, op, reads, writes, add=False):
                        S.op(eng, lambda h: h.tensor_tensor(out=out, in0=a, in1=b_, op=op), reads=reads, writes=writes, add=add)

                    def _cb_b(cb):
                        nonlocal kz, kv, kdf, kyo, khp
                        t_vf = Tl()
                        S.epoch(t_vTM)
                        for tq in range(ntq):
                            zc = zoff[si_] + tq * 512
                            for cc in range(4):
                                ch = cb * 4 + cc
                                z1 = kz % 4
                                z2 = (kz + 1) % 4
                                kz += 2
                                S.dma("sp", lambda h, z1=z1, ch=ch, zc=zc: h.dma_start(out=zh[z1][:], in_=zscr[(16 + ch) * 128:(17 + ch) * 128, zc:zc + 514]), writes=[t_zh[z1]])
                                S.dma("sp", lambda h, z2=z2, ch=ch, zc=zc: h.dma_start(out=zh[z2][:], in_=zscr[(32 + ch) * 128:(33 + ch) * 128, zc:zc + 514]), writes=[t_zh[z2]])
                                c1 = kv % 4
                                c2 = (kv + 1) % 4
                                kv += 2
                                conv(z1, 16 + ch, cv[c1][:], [t_cv[c1]])
                                conv(z2, 32 + ch, cv[c2][:], [t_cv[c2]])
                                S.op("pool", lambda h, c1=c1, c2=c2: h.tensor_tensor(out=cv[c1][:], in0=cv[c1][:], in1=cv[c2][:], op=ALU.mult), reads=[t_cv[c1], t_cv[c2]], writes=[t_cv[c1]])
                                S.dma("pool", lambda h, c1=c1, ch=ch, tq=tq: h.dma_start(out=vfm[ch * 128:(ch + 1) * 128, s0 + tq * 512:s0 + (tq + 1) * 512], in_=cv[c1][:]), reads=[t_cv[c1]], writes=[t_vf], add=True)
                                iv = (kv // 2) % 2
                                S.op("act", lambda h, iv=iv, c1=c1: h.copy(out=vb[iv][:, 0:256], in_=cv[c1][:, 0:512:2]), reads=[t_cv[c1]], writes=[t_vb[iv]])
                                S.op("act", lambda h, iv=iv, c1=c1: h.copy(out=vb[iv][:, 256:512], in_=cv[c1][:, 1:512:2]), reads=[t_cv[c1]], writes=[t_vb[iv]], add=True)
                                b = bank()
                                pbf = psb[b][:].bitcast(BF16)
                                fns = [lambda h, k=k, iv=iv, pbf=pbf: h.transpose(pbf[:, k * 128:(k + 1) * 128], vb[iv][:, k * 128:(k + 1) * 128], identb[:]) for k in range(4)]
                                S.grp("pe", fns, reads=[t_vb[iv], t_const], writes=[t_ps[b]])
                                copy_op(evac_eng(), vTM[:].rearrange("p (g b) n -> p g b n", g=2)[:, :, tq * 2:tq * 2 + 2, cc * 128:(cc + 1) * 128],
                                        pbf[:, 0:512].rearrange("p (g b n) -> p g b n", g=2, b=2), [t_ps[b]], [t_vTM], add=True)
                        if "x_nofwd" in dbg:
                            return
                        S.epoch(t_Y)
                        for fb in range(nh + 1):
                            half = fb == nh
                            hP = hPs[khp % 2]
                            t_hP = t_hPs[khp % 2]
                            khp += 1
                            if not half:
                                i = kdf % 2
                                kdf += 1
                                cfv = dft[i][:, 0, 0:nfb * 128].rearrange("p (tb f) -> p tb f", f=128)
                                sfv = dft[i][:, 1, 0:nfb * 128].rearrange("p (tb f) -> p tb f", f=128)
                                load_fwd(cfv, sfv, t_dft[i], fb)
                                S.dma("sp", lambda h, fb=fb, hP=hP: h.dma_start(out=hP[:], in_=Hs[:, fb * 128:(fb + 1) * 128, cb * 512:(cb + 1) * 512].rearrange("a p n -> p a n")), writes=[t_hP])
                                bks = [bank() for _ in range(4)]
                                for q, (mt, off) in enumerate(((cfv, 0), (cfv, nh), (sfv, 0), (sfv, nh))):
                                    fns = [lambda h, tb=tb, b=bks[q], mt=mt, off=off: h.matmul(psb[b][:], mt[:, off + tb, :], vTM[:, off + tb, :], start=(tb == 0), stop=(tb == nh - 1)) for tb in range(nh)]
                                    S.grp("pe", fns, reads=[t_dft[i], t_vTM], writes=[t_ps[bks[q]]])
                                zA, zB = zh[0][:, 0:512], zh[1][:, 0:512]
                                S.op("act", lambda h, b=bks[1], zA=zA: h.copy(out=zA, in_=psb[b][:]), reads=[t_ps[bks[1]]], writes=[t_zh[0]])
                                S.op("act", lambda h, b=bks[3], zB=zB: h.copy(out=zB, in_=psb[b][:]), reads=[t_ps[bks[3]]], writes=[t_zh[1]])
                                Alo, Ahi, Blo, Bhi = cv[0][:], cv[1][:], cv[2][:], cv[3][:]
                                tt("dve", Alo, psb[bks[0]][:], zA, ALU.add, [t_ps[bks[0]], t_zh[0]], [t_cv[0]])
                                tt("dve", Ahi, psb[bks[0]][:], zA, ALU.subtract, [t_ps[bks[0]], t_zh[0]], [t_cv[1]])
                                tt("dve", Blo, psb[bks[2]][:], zB, ALU.add, [t_ps[bks[2]], t_zh[1]], [t_cv[2]])
                                tt("dve", Bhi, psb[bks[2]][:], zB, ALU.subtract, [t_ps[bks[2]], t_zh[1]], [t_cv[3]])
                                for (eng, A_, B_, tA_, tB_, Ps, Qs, o_r, o_i, s1, s2, ts1, ts2, two) in (
                                        ("dve", Alo, Blo, t_cv[0], t_cv[2], hP[:, 0, :], hP[:, 1, :], wk[0][:], wk[1][:], zh[2][:, 0:512], zh[3][:, 0:512], t_zh[2], t_zh[3], t_wkd),
                                        ("pool", Ahi, Bhi, t_cv[1], t_cv[3], hP[:, 2, :], hP[:, 3, :], wk[2][:], wk[3][:], wk[4][:], wk[5][:], t_wk4, t_wk5, t_wkp)):
                                    tt(eng, s1, A_, Ps, ALU.mult, [tA_, t_hP], [ts1])
                                    tt(eng, s2, B_, Qs, ALU.mult, [tB_, t_hP], [ts2])
                                    tt(eng, o_r, s1, s2, ALU.subtract, [ts1, ts2], [two])
                                    tt(eng, s1, A_, Qs, ALU.mult, [tA_, t_hP], [ts1])
                                    tt(eng, s2, B_, Ps, ALU.mult, [tB_, t_hP], [ts2])
                                    tt(eng, o_i, s1, s2, ALU.add, [ts1, ts2], [two], add=True)
                                tt("dve", G[0][:, fb, :], wk[0][:], wk[2][:], ALU.add, [t_wkd, t_wkp], [t_Y], add=True)
                                tt("dve", G[2][:, fb, :], wk[0][:], wk[2][:], ALU.subtract, [t_wkd, t_wkp], [t_Y], add=True)
                                tt("pool", G[1][:, fb, :], wk[1][:], wk[3][:], ALU.add, [t_wkd, t_wkp], [t_Y], add=True)
                                tt("pool", G[3][:, fb, :], wk[1][:], wk[3][:], ALU.subtract, [t_wkd, t_wkp], [t_Y], add=True)
                            else:
                                bA = bank()
                                bB = bank()
                                S.dma("sp", lambda h, hP=hP: h.dma_start(out=hP[0:1, 0:2, :], in_=Hs[0:2, Lh:Lh + 1, cb * 512:(cb + 1) * 512].rearrange("a p n -> p a n")), writes=[t_hP])
                                fns = [lambda h, tb=tb, b=bA: h.matmul(psb[b][0:1, :], altc[:, 0:1], vTM[:, tb, :], start=(tb == 0), stop=(tb == nh - 1)) for tb in range(nh)]
                                S.grp("pe", fns, reads=[t_h0, t_vTM], writes=[t_ps[bA]])
                                fns = [lambda h, tb=tb, b=bB: h.matmul(psb[b][0:1, :], altc[:, 0:1], vTM[:, nh + tb, :], start=(tb == 0), stop=(tb == nh - 1)) for tb in range(nh)]
                                S.grp("pe", fns, reads=[t_h0, t_vTM], writes=[t_ps[bB]])
                                s_ap, s_bq, s_aq, s_bp = wk[4][0:1, :], wk[5][0:1, :], zh[2][0:1, 0:512], zh[3][0:1, 0:512]
                                tt("dve", s_ap, psb[bA][0:1, :], hP[0:1, 0, :], ALU.mult, [t_ps[bA], t_hP], [t_wk4])
                                tt("dve", s_bq, psb[bB][0:1, :], hP[0:1, 1, :], ALU.mult, [t_ps[bB], t_hP], [t_wk5])
                                tt("dve", s_aq, psb[bA][0:1, :], hP[0:1, 1, :], ALU.mult, [t_ps[bA], t_hP], [t_zh[2]])
                                tt("dve", s_bp, psb[bB][0:1, :], hP[0:1, 0, :], ALU.mult, [t_ps[bB], t_hP], [t_zh[3]])
                                tt("dve", YLh[:, 0, :], s_ap, s_bq, ALU.subtract, [t_wk4, t_wk5], [t_Y], add=True)
                                tt("dve", YLh[:, 1, :], s_aq, s_bp, ALU.add, [t_zh[2], t_zh[3]], [t_Y], add=True)
                        if "x_noinv" in dbg:
                            return
                        FG = 4
                        nfg = nh // FG
                        for sp in range(L // 1024):
                            for ccp in range(2):
                                bk = {}
                                for cc in (2 * ccp, 2 * ccp + 1):
                                    bk[(cc, 0)] = bank()
                                    bk[(cc, 1)] = bank()
                                for fg in range(nfg):
                                    i = kdf % 2
                                    kdf += 1
                                    mv = dft[i][:].rearrange("p a (b f t) -> p (a b) f t", b=2, f=FG)
                                    for mi, mat in enumerate((CE, SE, COT, SOT)):
                                        S.dma("sp", lambda h, mv=mv, mi=mi, mat=mat, fg=fg, sp=sp: h.dma_start(out=mv[:, mi, :, :], in_=mat[fg * FG * 128:(fg + 1) * FG * 128, sp * 512:(sp + 1) * 512].rearrange("(fb p) t -> p fb t", p=128)),
                                              writes=[t_dft[i]], add=(mi > 0))
                                    for cc in (2 * ccp, 2 * ccp + 1):
                                        for par in (0, 1):
                                            b = bk[(cc, par)]
                                            fns = []
                                            for f in range(FG):
                                                fb = fg * FG + f
                                                fns.append(lambda h, b=b, fb=fb, f=f, cc=cc, mv=mv, par=par, st_=(fg == 0 and f == 0): h.matmul(psb[b][:], G[2 * par][:, fb, cc * 128:(cc + 1) * 128], mv[:, 2 * par, f, :], start=st_, stop=False))
                                                fns.append(lambda h, b=b, fb=fb, f=f, cc=cc, mv=mv, par=par: h.matmul(psb[b][:], G[2 * par + 1][:, fb, cc * 128:(cc + 1) * 128], mv[:, 2 * par + 1, f, :], start=False, stop=False))
                                            if fg == nfg - 1:
                                                fns.append(lambda h, b=b, cc=cc, par=par: h.matmul(psb[b][:], YLh[0:1, par, cc * 128:(cc + 1) * 128], altr[0:1, :], start=False, stop=True))
                                            S.grp("pe", fns, reads=[t_dft[i], t_Y, t_h0], writes=[t_ps[b]])
                                for cc in (2 * ccp, 2 * ccp + 1):
                                    ch = cb * 4 + cc
                                    for hlf in range(2):
                                        tq = sp * 2 + hlf
                                        zc = zoff[si_] + tq * 512
                                        z1 = kz % 4
                                        kz += 1
                                        S.dma("sp", lambda h, z1=z1, ch=ch, zc=zc: h.dma_start(out=zh[z1][:], in_=zscr[ch * 128:(ch + 1) * 128, zc:zc + 514]), writes=[t_zh[z1]])
                                        c1 = kv % 4
                                        c2 = (kv + 1) % 4
                                        kv += 2
                                        conv(z1, ch, cv[c1][:], [t_cv[c1]])
                                        S.dma("sp", lambda h, c2=c2, ch=ch, tq=tq: h.dma_start(out=cv[c2][:], in_=vfm[ch * 128:(ch + 1) * 128, s0 + tq * 512:s0 + (tq + 1) * 512]), reads=[t_vf], writes=[t_cv[c2]])
                                        for par in (0, 1):
                                            b = bk[(cc, par)]
                                            S.op("dve", lambda h, c2=c2, ch=ch, b=b, par=par, hlf=hlf: h.scalar_tensor_tensor(out=cv[c2][:, par:512:2], in0=cv[c2][:, par:512:2], scalar=skp[:, ch:ch + 1], in1=psb[b][:, hlf * 256:(hlf + 1) * 256], op0=ALU.mult, op1=ALU.add),
                                                 reads=[t_cv[c2], t_ps[b], t_h0], writes=[t_cv[c2]])
                                        io = kyo % 2
                                        kyo += 1
                                        S.op("dve", lambda h, c1=c1, c2=c2, io=io: h.tensor_tensor(out=yo[io][:], in0=cv[c2][:], in1=cv[c1][:], op=ALU.mult), reads=[t_cv[c1], t_cv[c2]], writes=[t_yo[io]])
                                        S.dma("pool", lambda h, io=io, ch=ch, tq=tq: h.dma_start(out=oT[ch * 128:(ch + 1) * 128, s0 + tq * 512:s0 + (tq + 1) * 512], in_=yo[io][:]), reads=[t_yo[io]])
                                        if cast_late:
                                            cast_late.pop(0)()
                    for cb in range(4):
                        _cb_b(cb)
                    S.barrier()
            for si_, L in enumerate(SEQS):
                hyena_seq(si_, L)
            while cast_late:
                cast_late.pop(0)()
            S.barrier()
        if "stop4a" in dbg:
            S.emit()
            return nc
        if "stop4" in dbg:
            S.emit()
            return nc
        chain(1)
        S.emit()
    return nc


_CACHE = {}


def _consts(SEQS, inp):
    c = {}
    c["k_ident"] = np.eye(128, dtype=np.float32)
    t5t, t5f = t5_tiles(np.asarray(inp["rel_bias_table"], np.float32))
    c["k_t5t"] = t5t
    c["k_t5f"] = t5f
    c["k_nat"] = na_tiles(np.asarray(inp["na_bias"], np.float32)[0])
    pj = np.arange(128)[:, None] + np.arange(512)[None, :]
    c["k_alt"] = (1.0 - 2.0 * (pj % 2)).astype(np.float32)
    c["k_delta"] = np.ascontiguousarray(np.broadcast_to(decay_deltas()[None, :], (128, D))).astype(np.float32)
    for L in sorted(set(SEQS)):
        for k, mtx in enumerate(dft_mats(L)):
            c["k_dft%d_%d" % (L, k)] = mtx
        ft, negt = filt_consts(L)
        c["k_feat%d" % L] = ft
        c["k_negt%d" % L] = negt
        nh = L // 256
        wn = np.full((L // 2,), 2.0 / (2 * L), np.float32)
        wn[0] = 1.0 / (2 * L)
        c["k_wn%d" % L] = np.ascontiguousarray(wn.reshape(nh, 128).T)
    return c


WNAMES = ["norm_mix", "norm_ffn", "norm_ple", "final_norm", "ab_w_in", "ab_w_out", "diff_lambda", "diff_subln",
          "c_w_in", "c_conv_w", "c_conv_b", "c_filt_w1", "c_filt_b1", "c_filt_freq", "c_filt_w2", "c_filt_b2",
          "c_filt_w3", "c_skip", "c_w_out", "ffn_w_in", "ffn_w_out", "ple_w_proj", "ple_w_gate"]


def run_cores(SEQS, core_inputs, inp, dbg=()):
    key = (tuple(SEQS), tuple(dbg))
    if key not in _CACHE:
        _CACHE[key] = build_program(list(SEQS), dbg)
    nc = _CACHE[key]
    shared = {k: np.ascontiguousarray(np.asarray(inp[k], np.float32)) for k in WNAMES}
    shared.update(_consts(SEQS, inp))
    in_maps = []
    for xs, pp in core_inputs:
        m = dict(shared)
        m["xs"] = np.ascontiguousarray(xs, np.float32)
        m["pp"] = np.ascontiguousarray(pp, np.float32)
        in_maps.append(m)
    res = run_bass_kernel_spmd(nc, in_maps, core_ids=list(range(len(in_maps))))
    return res.results


def kernel(**inp):
    xp = np.asarray(inp["x_prompt"], np.float32)
    xsm = np.asarray(inp["x_sample"], np.float32)
    ppr = np.asarray(inp["p_prompt"], np.float32)
    psm = np.asarray(inp["p_sample"], np.float32)
    B, LP, _ = xp.shape
    LSm = xsm.shape[1]
    SEQS = [LP, LSm]
    core_inputs = []
    for i in range(B):
        xs = np.concatenate([xp[i], xsm[i]], axis=0)
        pp = np.concatenate([ppr[:, i], psm[:, i]], axis=1)
        core_inputs.append((xs, pp))
    res = run_cores(SEQS, core_inputs, inp)
    yp = np.stack([res[i]["y"][:LP] for i in range(B)], axis=0).astype(np.float32)
    ys = np.stack([res[i]["y"][LP:] for i in range(B)], axis=0).astype(np.float32)
    return (yp, ys)
```

```python
import math
from contextlib import ExitStack
import numpy as np
import ml_dtypes
import concourse.bass as bass
import concourse.mybir as mybir
from concourse.bass_utils import run_bass_kernel_spmd

F32 = mybir.dt.float32
BF16 = mybir.dt.bfloat16
I32 = mybir.dt.int32
AF = mybir.ActivationFunctionType
ALU = mybir.AluOpType
AX = mybir.AxisListType

D = 2048
NCH = 16
DFF = 5632
T = 512
EPS = 1e-6
NEG = -30000.0


class Tl:
    __slots__ = ("w", "r", "p")

    def __init__(self):
        self.w = {}
        self.r = {}
        self.p = {}


class Sched:
    ENG = ("pe", "act", "dve", "pool", "sp")

    def __init__(self, nc, stack, n_dma_sems=8):
        self.nc = nc
        self.sems = []
        self.cnt = []
        self.esem = {}
        self.ops = {e: [] for e in self.ENG}
        self.seen = {e: {} for e in self.ENG}
        for e in self.ENG:
            self.esem[e] = self._new_sem(stack, "s_" + e)
        self.dpool = {}
        self.dnext = {}
        for q in ("sp", "pool", "act"):
            self.dpool[q] = [self._new_sem(stack, "d_%s%d" % (q, i)) for i in range(n_dma_sems)]
            self.dnext[q] = 0

    def _new_sem(self, stack, name):
        s = stack.enter_context(self.nc.semaphore(name))
        self.sems.append(s)
        self.cnt.append(0)
        return len(self.sems) - 1

    def _waits(self, eng, reads, writes, extra=(), addw=()):
        need = {}
        for t in reads:
            for d in (t.w, t.p):
                for s, v in d.items():
                    if need.get(s, 0) < v:
                        need[s] = v
        for t in writes:
            for d in (t.w, t.r, t.p):
                for s, v in d.items():
                    if need.get(s, 0) < v:
                        need[s] = v
        for t in addw:
            for s, v in t.p.items():
                if need.get(s, 0) < v:
                    need[s] = v
        for s, v in extra:
            if need.get(s, 0) < v:
                need[s] = v
        seen = self.seen[eng]
        own = self.esem[eng]
        out = []
        for s, v in need.items():
            if eng == "pe" and s == own:
                continue
            if seen.get(s, 0) >= v:
                continue
            seen[s] = v
            out.append((s, v))
        return out

    def _mark(self, ev, reads, writes, add):
        s, v = ev
        for t in reads:
            if t.r.get(s, 0) < v:
                t.r[s] = v
        for t in writes:
            if add:
                t.w[s] = v
            else:
                t.w = {s: v}
                t.r = {}
                t.p = {}

    def epoch(self, t):
        p = dict(t.p)
        for d in (t.w, t.r):
            for s, v in d.items():
                if p.get(s, 0) < v:
                    p[s] = v
        t.p = p
        t.w = {}
        t.r = {}

    def op(self, eng, fn, reads=(), writes=(), add=False):
        waits = self._waits(eng, reads, () if add else writes, (), writes if add else ())
        s = self.esem[eng]
        self.cnt[s] += 1
        ev = (s, self.cnt[s])
        self.ops[eng].append((waits, fn, (s, 1)))
        self._mark(ev, reads, writes, add)

    def grp(self, eng, fns, reads=(), writes=()):
        waits = self._waits(eng, reads, writes)
        s = self.esem[eng]
        self.cnt[s] += 1
        ev = (s, self.cnt[s])
        n = len(fns)
        for i, fn in enumerate(fns):
            self.ops[eng].append((waits if i == 0 else (), fn, (s, 1) if i == n - 1 else None))
        self._mark(ev, reads, writes, False)

    def dma(self, q, fn, reads=(), writes=(), add=False):
        pool = self.dpool[q]
        i = self.dnext[q]
        self.dnext[q] = (i + 1) % len(pool)
        s = pool[i]
        extra = [(s, self.cnt[s])] if self.cnt[s] > 0 else []
        waits = self._waits(q, reads, () if add else writes, extra, writes if add else ())
        self.cnt[s] += 16
        ev = (s, self.cnt[s])
        self.ops[q].append((waits, fn, (s, 16)))
        self._mark(ev, reads, writes, add)

    def barrier(self):
        allv = [(s, v) for s, v in enumerate(self.cnt) if v > 0]
        for e in self.ENG:
            seen = self.seen[e]
            w = []
            for s, v in allv:
                if seen.get(s, 0) < v:
                    seen[s] = v
                    w.append((s, v))
            if w:
                self.ops[e].append((w, None, None))

    def emit(self):
        nc = self.nc
        sems = self.sems
        with nc.Block() as b0:
            @b0.vector
            def _(v):
                for s in sems:
                    v.sem_clear(s)
        with nc.Block() as blk:
            def run(e):
                def body(h):
                    for waits, fn, inc in self.ops[e]:
                        for s, v in waits:
                            h.wait_ge(sems[s], v)
                        if fn is not None:
                            ins = fn(h)
                            if inc is not None:
                                ins.then_inc(sems[inc[0]], inc[1])
                return body
            blk.tensor(run("pe"))
            blk.scalar(run("act"))
            blk.vector(run("dve"))
            blk.gpsimd(run("pool"))
            blk.sync(run("sp"))


def t5_bucket_np(rel):
    nb = 16
    max_exact = 8
    ret = np.where(rel > 0, nb, 0)
    n = np.abs(rel)
    nf = np.maximum(n, 1).astype(np.float32)
    large = max_exact + (np.log(nf / np.float32(max_exact)) / np.float32(math.log(128 / max_exact))
                         * np.float32(nb - max_exact)).astype(np.int32)
    large = np.minimum(large, nb - 1)
    return ret + np.where(n < max_exact, n, large)


def t5_tiles(rel_table):
    kk = np.arange(128)[:, None, None]
    j = np.arange(6)[None, :, None]
    qq = np.arange(512)[None, None, :]
    rel = 128 * (j - 1) + kk - qq
    b = t5_bucket_np(rel.astype(np.int32))
    tiles = np.ascontiguousarray(np.transpose(rel_table[b], (3, 0, 1, 2))).astype(np.float32)
    far = np.stack([rel_table[15], rel_table[31]], axis=-1)
    far = np.ascontiguousarray(np.broadcast_to(far[:, None, :], (8, 128, 2))).astype(np.float32)
    return tiles, far


def na_tiles(na_table):
    out = np.empty((3, 8, 128, 8, 512), np.float32)
    rows = 64
    for ty, m in enumerate((0, 2, rows // 8 - 1)):
        ws = int(np.clip(8 * m - 4, 0, rows - 16))
        kk = np.arange(128)[:, None, None]
        j = np.arange(8)[None, :, None]
        qq = np.arange(512)[None, None, :]
        krow = ws + 2 * j + kk // 64
        kcol = kk % 64
        qrow = 8 * m + qq // 64
        qcol = qq % 64
        rs = np.clip(qrow - 4, 0, rows - 8)
        cs = np.clip(qcol - 8, 0, 64 - 16)
        valid = (krow >= rs) & (krow < rs + 8) & (kcol >= cs) & (kcol < cs + 16)
        dr = np.clip(krow - qrow + 7, 0, 14)
        dc = np.clip(kcol - qcol + 15, 0, 30)
        dr, dc, valid = np.broadcast_arrays(dr, dc, valid)
        g = na_table[:, dr, dc]
        out[ty] = np.where(valid[None], g, np.float32(NEG))
    return out


def dft_mats(L):
    N = 2 * L
    Lh = L // 2
    tau = np.arange(Lh, dtype=np.int64)[:, None]
    f = np.arange(Lh, dtype=np.int64)[None, :]
    out = []
    for tt_ in (2 * tau, 2 * tau + 1):
        m = (tt_ * f) % N
        ang = m.astype(np.float64) * (2.0 * np.pi / N)
        out.append((np.cos(ang).astype(ml_dtypes.bfloat16), np.sin(ang).astype(ml_dtypes.bfloat16)))
    (ce, se), (co, so) = out
    return [ce, se, co, so, np.ascontiguousarray(co.T), np.ascontiguousarray(so.T)]


def filt_consts(L):
    t = np.linspace(0.0, 1.0, L, dtype=np.float32)[:, None]
    bands = 16
    w = (2.0 * math.pi * np.arange(L, dtype=np.float32) / L).astype(np.float32)
    f = np.linspace(1e-4, bands - 1, bands, dtype=np.float32)
    ang = w[:, None] * f[None, :]
    feats = np.concatenate([t, np.cos(ang), -np.sin(ang)], axis=-1).astype(np.float32)
    nh = L // 256
    tv = -t[:, 0]
    ev = tv[0::2].reshape(nh, 128).T
    od = tv[1::2].reshape(nh, 128).T
    negt = np.concatenate([ev, od], axis=1)
    return np.ascontiguousarray(feats.T), np.ascontiguousarray(negt.astype(np.float32))


def decay_deltas():
    min_decay = math.log(1e-2) / 0.3
    max_decay = math.log(1e-2) / 1.5
    return np.abs(np.linspace(min_decay, max_decay, D, dtype=np.float32)).astype(np.float32)


W2D = {
    "ab_in": (D, 6144), "ab_out": (D, D), "c_in": (D, 6144), "c_out": (D, D),
    "ffn_in0": (D, 2 * DFF), "ffn_in1": (D, 2 * DFF), "ffn_out0": (DFF, D), "ffn_out1": (DFF, D),
    "gate0": (D, D), "gate1": (D, D), "proj0": (256, D), "proj1": (256, D),
}


def build_program(SEQS, dbg=()):
    nc = bass.Bass("TRN2", target_bir_lowering=False)
    NT = sum(SEQS)
    NTT = NT // T
    seq_off = [sum(SEQS[:i]) for i in range(len(SEQS))]
    zoff = [sum(l + 2 for l in SEQS[:i]) for i in range(len(SEQS))]
    NTZ = sum(l + 2 for l in SEQS)
    LS = sorted(set(SEQS))

    def din(name, shape, dt=F32):
        return nc.dram_tensor(name, list(shape), dt, kind="ExternalInput").ap()

    def dscr(name, shape, dt):
        kind = "ExternalOutput" if name in dbg else "Internal"
        return nc.dram_tensor(name, list(shape), dt, kind=kind).ap()

    xs = din("xs", [NT, D])
    pp = din("pp", [2, NT, 256])
    w_src = {
        "ab_in": din("ab_w_in", [1, D, 6144])[0], "ab_out": din("ab_w_out", [1, D, D])[0],
        "c_in": din("c_w_in", [1, D, 6144])[0], "c_out": din("c_w_out", [1, D, D])[0],
    }
    ffn_in = din("ffn_w_in", [2, D, 2 * DFF])
    ffn_out = din("ffn_w_out", [2, DFF, D])
    gate = din("ple_w_gate", [2, D, D])
    proj = din("ple_w_proj", [2, 256, D])
    for i in range(2):
        w_src["ffn_in%d" % i] = ffn_in[i]
        w_src["ffn_out%d" % i] = ffn_out[i]
        w_src["gate%d" % i] = gate[i]
        w_src["proj%d" % i] = proj[i]
    norm_mix = din("norm_mix", [2, D])
    norm_ffn = din("norm_ffn", [2, D])
    norm_ple = din("norm_ple", [2, D])
    final_norm = din("final_norm", [D])
    diff_lambda = din("diff_lambda", [1, 4, 64])
    diff_subln = din("diff_subln", [1, 128])
    c_conv_w = din("c_conv_w", [1, 3, 6144])
    c_conv_b = din("c_conv_b", [1, 6144])
    f_w1 = din("c_filt_w1", [1, 33, 64])
    f_b1 = din("c_filt_b1", [1, 64])
    f_freq = din("c_filt_freq", [1, 2, 64])
    f_w2 = din("c_filt_w2", [1, 64, 64])
    f_b2 = din("c_filt_b2", [1, 64])
    f_w3 = din("c_filt_w3", [1, 64, 2 * D])
    c_skip = din("c_skip", [1, D])
    ident_d = din("k_ident", [128, 128])
    t5t_d = din("k_t5t", [8, 128, 6, 512])
    t5f_d = din("k_t5f", [8, 128, 2])
    nat_d = din("k_nat", [3, 8, 128, 8, 512])
    delta_d = din("k_delta", [128, D])
    alt_d = din("k_alt", [128, 512])
    dftm = {L: [din("k_dft%d_%d" % (L, k), [L // 2, L // 2], BF16) for k in range(6)] for L in LS}
    feat_d = {L: din("k_feat%d" % L, [33, L]) for L in LS}
    negt_d = {L: din("k_negt%d" % L, [128, L // 128]) for L in LS}
    wn_d = {L: din("k_wn%d" % L, [128, L // 256]) for L in LS}
    y_out = nc.dram_tensor("y", [NT, D], F32, kind="ExternalOutput").ap()
    wb = {k: dscr("wb_" + k, shp, BF16) for k, shp in W2D.items()}
    hscr = dscr("hscr", [D, NT], F32)
    qkscr = dscr("qkscr", [4096, NT], BF16)
    vscr_a = dscr("vscr_a", [NT, 2048], BF16)
    oT = dscr("oT", [D, NT], BF16)
    zscr = dscr("zscr", [6144, NTZ], F32)
    vfm = dscr("vfm", [D, NT], F32)
    Hscr = {i: dscr("Hscr%d" % i, [4, SEQS[i] // 2 + 128, D], F32) for i in range(len(SEQS))}

    with ExitStack() as st:
        S = Sched(nc, st)

        uq = {"n": 0}

        def sb(stack, name, shape, dt):
            uq["n"] += 1
            return stack.enter_context(nc.sbuf_tensor("%s_%d" % (name, uq["n"]), list(shape), dt))

        psb = [st.enter_context(nc.psum_tensor("ps%d" % i, [128, 512], F32)) for i in range(8)]
        t_ps = [Tl() for _ in range(8)]
        rr = {"k": 0, "ev": 0}
        cast_late = []

        def bank(lo=0, hi=8):
            b = lo + rr["k"] % (hi - lo)
            rr["k"] += 1
            return b

        def evac_eng():
            rr["ev"] += 1
            return "act" if rr["ev"] % 2 else "dve"

        def copy_op(eng, out, in_, reads, writes, add=False):
            if eng == "act":
                S.op("act", lambda h: h.copy(out=out, in_=in_), reads=reads, writes=writes, add=add)
            else:
                S.op(eng, lambda h: h.tensor_copy(out=out, in_=in_), reads=reads, writes=writes, add=add)

        ident = sb(st, "ident", [128, 128], F32)
        identb = sb(st, "identb", [128, 128], BF16)
        ones = sb(st, "ones", [128, 128], BF16)
        cst = sb(st, "cst", [128, 8], F32)
        gains = sb(st, "gains", [128, 7, NCH], F32)
        lamt = sb(st, "lamt", [128, 4], F32)
        subg = sb(st, "subg", [128, 1], F32)
        t_const = Tl()
        S.dma("sp", lambda h: h.dma_start(out=ident[:], in_=ident_d[:, :]), writes=[t_const], add=True)
        S.op("pool", lambda h: h.memset(ones[:], 1.0), writes=[t_const], add=True)
        S.op("pool", lambda h: h.memset(cst[:, 0:1], EPS), writes=[t_const], add=True)
        S.op("pool", lambda h: h.memset(cst[:, 1:2], -3.14159), writes=[t_const], add=True)
        S.op("pool", lambda h: h.memset(cst[:, 2:3], 0.0), writes=[t_const], add=True)
        S.op("pool", lambda h: h.memset(cst[:, 3:4], 1.0), writes=[t_const], add=True)
        S.barrier()
        S.op("pool", lambda h: h.memset(cst[0:1, 3:4], 0.0), writes=[t_const], add=True)
        S.op("dve", lambda h: h.tensor_copy(out=identb[:], in_=ident[:]), reads=[t_const], writes=[t_const], add=True)
        gsrc = [norm_mix[0], norm_mix[1], norm_ffn[0], norm_ffn[1], norm_ple[0], norm_ple[1], final_norm]
        for i, g in enumerate(gsrc):
            S.dma("sp", lambda h, i=i, g=g: h.dma_start(out=gains[:, i, :], in_=g.rearrange("(c p) -> p c", p=128),
                                                        allow_slow_non_contiguous=True), writes=[t_const], add=True)
        S.dma("sp", lambda h: h.dma_start(out=subg[:], in_=diff_subln[0].rearrange("(p o) -> p o", o=1),
                                          allow_slow_non_contiguous=True), writes=[t_const], add=True)
        with ExitStack() as ph:
            lv = sb(ph, "lv", [128, 4, 64], F32)
            lp = sb(ph, "lp", [128, 2, 64], F32)
            ls = sb(ph, "ls", [128, 2], F32)
            le = sb(ph, "le", [128, 2], F32)
            t_lv = Tl()
            lam_src = bass.AP(diff_lambda.tensor, diff_lambda.offset, [[0, 128], [64, 4], [1, 64]])
            S.dma("sp", lambda h: h.dma_start(out=lv[:], in_=lam_src), writes=[t_lv])
            S.op("dve", lambda h: h.tensor_tensor(out=lp[:, 0, :], in0=lv[:, 0, :], in1=lv[:, 1, :], op=ALU.mult), reads=[t_lv], writes=[t_lv])
            S.op("dve", lambda h: h.tensor_tensor(out=lp[:, 1, :], in0=lv[:, 2, :], in1=lv[:, 3, :], op=ALU.mult), reads=[t_lv], writes=[t_lv])
            S.op("dve", lambda h: h.tensor_reduce(out=ls[:], in_=lp[:], axis=AX.X, op=ALU.add), reads=[t_lv], writes=[t_lv])
            S.op("act", lambda h: h.activation(out=le[:], in_=ls[:], func=AF.Exp), reads=[t_lv], writes=[t_lv])
            S.op("dve", lambda h: h.tensor_tensor(out=lamt[:, 0:1], in0=le[:, 0:1], in1=le[:, 1:2], op=ALU.subtract), reads=[t_lv], writes=[t_lv])
            S.op("dve", lambda h: h.tensor_scalar(out=lamt[:, 0:1], in0=lamt[:, 0:1], scalar1=0.2, scalar2=None, op0=ALU.add), reads=[t_lv], writes=[t_lv])
            S.op("dve", lambda h: h.tensor_scalar(out=lamt[:, 1:2], in0=lamt[:, 0:1], scalar1=-1.0, scalar2=None, op0=ALU.mult), reads=[t_lv], writes=[t_lv])
            def cast_thunks(names, CAST_ROWS=512):
                out = []
                for k in names:
                    K, N = W2D[k]
                    r0 = 0
                    while r0 < K:
                        nr = min(CAST_ROWS, K - r0)
                        out.append(lambda k=k, r0=r0, nr=nr: S.dma("pool", lambda h: h.dma_start(out=wb[k][r0:r0 + nr, :], in_=w_src[k][r0:r0 + nr, :]),
                                                                   writes=[t_const], add=True))
                        r0 += nr
                return out

            def cast_weights(names):
                for th in cast_thunks(names):
                    th()
            cast_weights(["ab_in"])
            S.barrier()

        if "stop0" in dbg:
            S.emit()
            return nc
        def rmsnorm(hT, t_hT, hn, t_hn, gi, sq, t_sq, rs, rstd, t_rs, tn=None, t_tn=None):
            b = bank()
            for c in range(NCH):
                i = c % 4
                if c % 2 == 0:
                    S.op("act", lambda h, c=c, i=i: h.activation(out=sq[i][:], in_=hT[:, c, :], func=AF.Square), reads=[t_hT], writes=[t_sq[i]])
                else:
                    S.op("pool", lambda h, c=c, i=i: h.tensor_tensor(out=sq[i][:], in0=hT[:, c, :], in1=hT[:, c, :], op=ALU.mult), reads=[t_hT], writes=[t_sq[i]])
                S.op("pe", lambda h, c=c, i=i, b=b: h.matmul(psb[b][:], ones[:], sq[i][:], start=(c == 0), stop=(c == NCH - 1)),
                     reads=[t_const, t_sq[i]], writes=[t_ps[b]])
            S.op("act", lambda h, b=b: h.activation(out=rs[:], in_=psb[b][:], func=AF.Sqrt, bias=cst[:, 0:1], scale=1.0 / D), reads=[t_ps[b], t_const], writes=[t_rs])
            S.op("dve", lambda h: h.reciprocal(out=rstd[:], in_=rs[:]), reads=[t_rs], writes=[t_rs])
            if hn is not None:
                S.epoch(t_hn)
                for c in range(NCH):
                    if c % 2 == 0 or tn is None:
                        S.op("dve", lambda h, c=c: h.scalar_tensor_tensor(out=hn[:, c, :], in0=hT[:, c, :], scalar=gains[:, gi, c:c + 1], in1=rstd[:],
                                                                          op0=ALU.mult, op1=ALU.mult), reads=[t_hT, t_const, t_rs], writes=[t_hn], add=True)
                    else:
                        k = (c // 2) % len(tn)
                        S.op("pool", lambda h, c=c, k=k: h.tensor_tensor(out=tn[k][:], in0=hT[:, c, :], in1=rstd[:], op=ALU.mult), reads=[t_hT, t_rs], writes=[t_tn[k]])
                        S.op("act", lambda h, c=c, k=k: h.activation(out=hn[:, c, :], in_=tn[k][:], func=AF.Copy, scale=gains[:, gi, c:c + 1]), reads=[t_tn[k], t_const], writes=[t_hn], add=True)

        def rmsnorm_pre(hT, t_hT, hn, t_hn, gi):
            S.epoch(t_hn)
            for c in range(NCH):
                if c % 2 == 0:
                    S.op("dve", lambda h, c=c: h.tensor_scalar(out=hn[:, c, :], in0=hT[:, c, :], scalar1=gains[:, gi, c:c + 1], scalar2=None, op0=ALU.mult),
                         reads=[t_hT, t_const], writes=[t_hn], add=True)
                else:
                    S.op("act", lambda h, c=c: h.activation(out=hn[:, c, :], in_=hT[:, c, :], func=AF.Copy, scale=gains[:, gi, c:c + 1]),
                         reads=[t_hT, t_const], writes=[t_hn], add=True)

        def rmsnorm_stats(hT, t_hT, sq, t_sq, rs, rstd, t_rs):
            b = bank()
            for c in range(NCH):
                i = c % 4
                if c % 2 == 0:
                    S.op("act", lambda h, c=c, i=i: h.activation(out=sq[i][:], in_=hT[:, c, :], func=AF.Square), reads=[t_hT], writes=[t_sq[i]])
                else:
                    S.op("pool", lambda h, c=c, i=i: h.tensor_tensor(out=sq[i][:], in0=hT[:, c, :], in1=hT[:, c, :], op=ALU.mult), reads=[t_hT], writes=[t_sq[i]])
                S.op("pe", lambda h, c=c, i=i, b=b: h.matmul(psb[b][:], ones[:], sq[i][:], start=(c == 0), stop=(c == NCH - 1)),
                     reads=[t_const, t_sq[i]], writes=[t_ps[b]])
            S.op("act", lambda h, b=b: h.activation(out=rs[:], in_=psb[b][:], func=AF.Sqrt, bias=cst[:, 0:1], scale=1.0 / D), reads=[t_ps[b], t_const], writes=[t_rs])
            S.op("dve", lambda h: h.reciprocal(out=rstd[:], in_=rs[:]), reads=[t_rs], writes=[t_rs])

        class WStream:
            def __init__(self, stack, nbuf=3):
                self.buf = [sb(stack, "wbuf%d" % i, [128, 16 * 512], BF16) for i in range(nbuf)]
                self.tl = [Tl() for _ in range(nbuf)]
                self.k = 0

            def load(self, specs):
                i = self.k % len(self.buf)
                self.k += 1
                kc = specs[0][1]
                W = sum(s[2] for s in specs)
                view = self.buf[i][:, 0:kc * W].rearrange("p (c n) -> p c n", n=W)
                o = 0
                for n, (ap, kc_, ncols) in enumerate(specs):
                    S.dma("sp", lambda h, ap=ap, o=o, ncols=ncols, view=view: h.dma_start(out=view[:, :, o:o + ncols], in_=ap.rearrange("(c p) n -> p c n", p=128)),
                          writes=[self.tl[i]], add=(n > 0))
                    o += ncols
                return view, self.tl[i]

        def transpose_in(src_tm, t_src, dst_fm, t_dst, nchunk, dt_ident, eng_alt=True):
            for c in range(nchunk):
                b = bank()
                fns = [lambda h, c=c, tb=tb, b=b: h.transpose(psb[b][:, tb * 128:(tb + 1) * 128], src_tm[:, tb, c * 128:(c + 1) * 128], dt_ident[:]) for tb in range(4)]
                S.grp("pe", fns, reads=[t_src, t_const], writes=[t_ps[b]])
                copy_op(evac_eng(), dst_fm[:, c, :], psb[b][:], [t_ps[b]], [t_dst], add=True)

        with ExitStack() as ph:
            WS = WStream(ph)
            xt = sb(ph, "xt", [128, 4, D], F32)
            hT = sb(ph, "hT", [128, NCH, T], F32)
            hn = sb(ph, "hn", [128, NCH, T], BF16)
            sq = [sb(ph, "sq%d" % i, [128, T], BF16) for i in range(4)]
            rs = sb(ph, "rs", [128, T], F32)
            rstd = sb(ph, "rstd", [128, T], F32)
            stg = [sb(ph, "stg%d" % i, [128, 4, T], BF16) for i in range(2)]
            tn1 = [sb(ph, "tn1_%d" % i, [128, T], F32) for i in range(2)]
            t_tn1 = [Tl(), Tl()]
            t_xt, t_hT, t_hn, t_rs = Tl(), Tl(), Tl(), Tl()
            t_sq = [Tl(), Tl(), Tl(), Tl()]
            t_stg = [Tl(), Tl()]
            sk = 0
            for tt in range(NTT):
                t0 = tt * T
                S.dma("sp", lambda h, t0=t0: h.dma_start(out=xt[:], in_=xs[t0:t0 + T, :].rearrange("(tb p) d -> p tb d", p=128)), writes=[t_xt])
                S.epoch(t_hT)
                transpose_in(xt, t_xt, hT, t_hT, NCH, ident)
                S.dma("pool", lambda h, t0=t0: h.dma_start(out=hscr.rearrange("(c p) t -> p c t", p=128)[:, :, t0:t0 + T], in_=hT[:]), reads=[t_hT])
                rmsnorm(hT, t_hT, hn, t_hn, 0, sq, t_sq, rs, rstd, t_rs, tn1, t_tn1)
                for nb in range(8):
                    c0 = nb * 512 if nb < 4 else 3072 + (nb - 4) * 512
                    wv, t_w = WS.load([(wb["ab_in"][:, c0:c0 + 512], NCH, 512)])
                    si = sk % 2
                    sk += 1
                    S.epoch(t_stg[si])
                    for j in range(4):
                        b = bank()
                        fns = [lambda h, c=c, j=j, b=b, wv=wv: h.matmul(psb[b][:], wv[:, c, j * 128:(j + 1) * 128], hn[:, c, :], start=(c == 0), stop=(c == NCH - 1)) for c in range(NCH)]
                        S.grp("pe", fns, reads=[t_w, t_hn], writes=[t_ps[b]])
                        copy_op(evac_eng(), stg[si][:, j, :], psb[b][:], [t_ps[b]], [t_stg[si]], add=True)
                    S.dma("pool", lambda h, nb=nb, si=si, t0=t0: h.dma_start(out=qkscr[nb * 512:(nb + 1) * 512, t0:t0 + T].rearrange("(j p) t -> p j t", p=128), in_=stg[si][:]),
                          reads=[t_stg[si]])
                for nb in range(4):
                    c0 = 2048 + nb * 512 if nb < 2 else 5120 + (nb - 2) * 512
                    wv, t_w = WS.load([(wb["ab_in"][:, c0:c0 + 512], NCH, 512)])
                    si = sk % 2
                    sk += 1
                    S.epoch(t_stg[si])
                    for tb in range(4):
                        b = bank()
                        fns = [lambda h, c=c, tb=tb, b=b, wv=wv: h.matmul(psb[b][:], hn[:, c, tb * 128:(tb + 1) * 128], wv[:, c, :], start=(c == 0), stop=(c == NCH - 1)) for c in range(NCH)]
                        S.grp("pe", fns, reads=[t_w, t_hn], writes=[t_ps[b]])
                        copy_op(evac_eng(), stg[si][:, tb, :], psb[b][:], [t_ps[b]], [t_stg[si]], add=True)
                    S.dma("pool", lambda h, nb=nb, si=si, t0=t0: h.dma_start(out=vscr_a[t0:t0 + T, nb * 512:(nb + 1) * 512].rearrange("(tb p) n -> p tb n", p=128), in_=stg[si][:]),
                          reads=[t_stg[si]])
            S.barrier()

        if "stop1" in dbg:
            S.emit()
            return nc
        with ExitStack() as ph:
            LM = max(SEQS)
            NB = 2
            qT = [sb(ph, "qT%d" % i, [128, LM], BF16) for i in range(NB)]
            kTa = [sb(ph, "kTa%d" % i, [128, LM], BF16) for i in range(NB)]
            kTb = [sb(ph, "kTb%d" % i, [128, LM], BF16) for i in range(NB)]
            kstate = [None] * NB
            vv = [sb(ph, "vv%d" % i, [128, LM // 128, 128], BF16) for i in range(NB)]
            t_in = [Tl() for _ in range(NB)]
            NBI = 3
            bia = [sb(ph, "bia%d" % i, [128, 8, 512], F32) for i in range(NBI)]
            t_bia = [Tl() for _ in range(NBI)]
            far = sb(ph, "far", [128, 8, 2], F32)
            pt = [sb(ph, "pt%d" % i, [128, 512], BF16) for i in range(4)]
            t_pt = [Tl() for _ in range(4)]
            tmp = [sb(ph, "tmp%d" % i, [128, 512], F32) for i in range(4)]
            t_tmp = [Tl() for _ in range(4)]
            NE = 2
            eo = [[sb(ph, "eo%d_%d" % (e, i), [128, 512], F32) for i in range(6)] for e in range(NE)]
            esq = [sb(ph, "esq%d" % e, [128, 512], BF16) for e in range(NE)]
            est = [sb(ph, "est%d" % e, [128, 512], BF16) for e in range(NE)]
            t_e = [Tl() for _ in range(NE)]
            S.dma("sp", lambda h: h.dma_start(out=far[:], in_=t5f_d.rearrange("h p c -> p h c")), writes=[t_const], add=True)
            S.barrier()
            L0_NAMES = ["ab_out", "ffn_in0", "ffn_out0", "gate0", "proj0", "c_in"]
            cast_list = cast_thunks(L0_NAMES, 128)
            cast_late.extend(cast_thunks([k for k in W2D if k != "ab_in" and k not in L0_NAMES], 128))

            heads = []
            for si_, L in enumerate(SEQS):
                for hh in range(16):
                    heads.append((si_, L, seq_off[si_], hh))
            segs = []
            for hi, (si_, L, s0, hh) in enumerate(heads):
                if hh < 8:
                    segs.append((hi, "A", 0))
                else:
                    for ty in range(3):
                        segs.append((hi, "N", ty))
            seg_of = {(hi, ty): k for k, (hi, kind, ty) in enumerate(segs)}

            def load_head(hi):
                si_, L, s0, hh = heads[hi]
                ib = hi % NB
                isA = hh < 8
                h8 = hh % 8
                nkb = L // 128
                qrow = (0 if isA else 2048) + h8 * 128
                krow = qrow + 1024
                vcol = (0 if isA else 1024) + h8 * 128
                S.epoch(t_in[ib])
                if isA and kstate[ib] != "A":
                    S.op("dve", lambda h: h.memset(kTa[ib][64:128, :], 0.0), writes=[t_in[ib]], add=True)
                    S.op("dve", lambda h: h.memset(kTb[ib][0:64, :], 0.0), writes=[t_in[ib]], add=True)
                kstate[ib] = "A" if isA else "N"
                S.dma("sp", lambda h: h.dma_start(out=qT[ib][:, 0:L], in_=qkscr[qrow:qrow + 128, s0:s0 + L]), writes=[t_in[ib]], add=True)
                if isA:
                    S.dma("sp", lambda h: h.dma_start(out=kTa[ib][0:64, 0:L], in_=qkscr[krow:krow + 64, s0:s0 + L]), writes=[t_in[ib]], add=True)
                    S.dma("sp", lambda h: h.dma_start(out=kTb[ib][64:128, 0:L], in_=qkscr[krow + 64:krow + 128, s0:s0 + L]), writes=[t_in[ib]], add=True)
                else:
                    S.dma("sp", lambda h: h.dma_start(out=kTa[ib][:, 0:L], in_=qkscr[krow:krow + 128, s0:s0 + L]), writes=[t_in[ib]], add=True)
                for k8 in range(nkb // 8):
                    S.dma("sp", lambda h, k8=k8: h.dma_start(out=vv[ib][:, k8 * 8:(k8 + 1) * 8, :], in_=vscr_a[s0 + k8 * 1024:s0 + (k8 + 1) * 1024, vcol:vcol + 128].rearrange("(kb p) n -> p kb n", p=128)),
                          writes=[t_in[ib]], add=True)

            def load_seg(k):
                hi, kind, ty = segs[k]
                h8 = heads[hi][3] % 8
                bi = k % NBI
                if kind == "A":
                    S.dma("sp", lambda h: h.dma_start(out=bia[bi][:, 0:6, :], in_=t5t_d[h8]), writes=[t_bia[bi]])
                else:
                    S.dma("sp", lambda h: h.dma_start(out=bia[bi][:], in_=nat_d[ty, h8]), writes=[t_bia[bi]])

            U = []
            for hi, (si_, L, s0, hh) in enumerate(heads):
                isA = hh < 8
                nkb = L // 128
                nq = L // 512
                for Q in range(nq):
                    if isA:
                        kbs = list(range(nkb))
                        ty = 0
                    else:
                        rows = L // 64
                        ws = int(np.clip(8 * Q - 4, 0, rows - 16))
                        kbs = [ws // 2 + j for j in range(8)]
                        ty = 0 if Q == 0 else (2 if Q == nq - 1 else 1)
                    for ki, kb in enumerate(kbs):
                        for m in range(2 if isA else 1):
                            U.append(dict(hi=hi, Q=Q, ki=ki, kb=kb, m=m, ty=ty, first=(ki == 0), last=(ki == len(kbs) - 1),
                                          lastQ=(ki == len(kbs) - 1 and m == (1 if isA else 0)),
                                          newhead=(Q == 0 and ki == 0 and m == 0), newseg=(ki == 0 and m == 0)))
            cnt = {"e": 0}
            pend = []

            def stageA(i, u):
                si_, L, s0, hh = heads[u["hi"]]
                isA = hh < 8
                ib = u["hi"] % NB
                b = 4 + i % 4
                kb, Q, m = u["kb"], u["Q"], u["m"]
                kk_ = kTb[ib] if (isA and m == 1) else kTa[ib]
                S.op("pe", lambda h: h.matmul(psb[b][:], kk_[:, kb * 128:(kb + 1) * 128], qT[ib][:, Q * 512:(Q + 1) * 512], start=True, stop=True),
                     reads=[t_in[ib]], writes=[t_ps[b]])

            def stageB(i, u):
                si_, L, s0, hh = heads[u["hi"]]
                isA = hh < 8
                h8 = hh % 8
                b = 4 + i % 4
                ip = i % 4
                scale = 0.125 if isA else 128 ** -0.5
                jj = (u["kb"] - 4 * u["Q"] + 1) if isA else u["ki"]
                bi = seg_of[(u["hi"], u["ty"])] % NBI
                if (not isA) or (0 <= jj <= 5):
                    S.op("dve", lambda h: h.scalar_tensor_tensor(out=tmp[ip][:], in0=psb[b][:], scalar=scale, in1=bia[bi][:, jj, :], op0=ALU.mult, op1=ALU.add),
                         reads=[t_ps[b], t_bia[bi]], writes=[t_tmp[ip]])
                    S.op("act", lambda h: h.activation(out=pt[ip][:], in_=tmp[ip][:], func=AF.Exp), reads=[t_tmp[ip]], writes=[t_pt[ip]])
                else:
                    fc = 0 if jj < 0 else 1
                    S.op("act", lambda h: h.activation(out=pt[ip][:], in_=psb[b][:], func=AF.Exp, bias=far[:, h8, fc:fc + 1], scale=scale),
                         reads=[t_ps[b], t_const], writes=[t_pt[ip]])

            def stageC(i, u):
                ib = u["hi"] % NB
                ip = i % 4
                kb, m = u["kb"], u["m"]
                first, last = u["first"], u["last"]
                bo, bl = 2 * m, 2 * m + 1
                S.op("pe", lambda h: h.matmul(psb[bo][:], vv[ib][:, kb, :], pt[ip][:], start=first, stop=last), reads=[t_in[ib], t_pt[ip]], writes=[t_ps[bo]])
                S.op("pe", lambda h: h.matmul(psb[bl][:], ones[:], pt[ip][:], start=first, stop=last), reads=[t_const, t_pt[ip]], writes=[t_ps[bl]])

            def epilogue(i, u):
                si_, L, s0, hh = heads[u["hi"]]
                isA = hh < 8
                Q = u["Q"]
                e = cnt["e"] % NE
                cnt["e"] += 1
                o = eo[e]
                te = t_e[e]
                S.epoch(te)
                S.op("dve", lambda h: h.tensor_copy(out=o[1][:], in_=psb[1][:]), reads=[t_ps[1]], writes=[te], add=True)
                S.op("act", lambda h: h.copy(out=o[0][:], in_=psb[0][:]), reads=[t_ps[0]], writes=[te], add=True)
                if isA:
                    S.op("dve", lambda h: h.tensor_copy(out=o[3][:], in_=psb[3][:]), reads=[t_ps[3]], writes=[te], add=True)
                    S.op("act", lambda h: h.copy(out=o[2][:], in_=psb[2][:]), reads=[t_ps[2]], writes=[te], add=True)

                def store():
                    S.dma("pool", lambda h: h.dma_start(out=oT[hh * 128:(hh + 1) * 128, s0 + Q * 512:s0 + (Q + 1) * 512], in_=est[e][:]), reads=[te])
                    if si_ == len(SEQS) - 1:
                        for _ in range(4):
                            if cast_list:
                                cast_list.pop(0)()

                if isA:
                    def e2a():
                        S.op("act", lambda h: h.activation(out=o[1][:], in_=o[1][:], func=AF.Ln), reads=[te], writes=[te])
                        S.op("act", lambda h: h.activation(out=o[1][:], in_=o[1][:], func=AF.Exp, scale=-1.0), reads=[te], writes=[te])
                        S.op("act", lambda h: h.activation(out=o[3][:], in_=o[3][:], func=AF.Ln), reads=[te], writes=[te])
                        S.op("act", lambda h: h.activation(out=o[3][:], in_=o[3][:], func=AF.Exp, scale=-1.0), reads=[te], writes=[te])
                        S.op("dve", lambda h: h.tensor_tensor(out=o[0][:], in0=o[0][:], in1=o[1][:], op=ALU.mult), reads=[te], writes=[te])
                        S.op("dve", lambda h: h.tensor_tensor(out=o[2][:], in0=o[2][:], in1=o[3][:], op=ALU.mult), reads=[te], writes=[te])
                        S.op("dve", lambda h: h.scalar_tensor_tensor(out=o[4][:], in0=o[2][:], scalar=lamt[:, 1:2], in1=o[0][:], op0=ALU.mult, op1=ALU.add), reads=[te, t_const], writes=[te])
                        S.op("dve", lambda h: h.tensor_tensor(out=esq[e][:], in0=o[4][:], in1=o[4][:], op=ALU.mult), reads=[te], writes=[te])
                        b = 4 + (i + 1) % 4
                        S.op("pe", lambda h: h.matmul(psb[b][:], ones[:], esq[e][:], start=True, stop=True), reads=[te, t_const], writes=[t_ps[b]])
                        S.op("act", lambda h: h.activation(out=o[5][:], in_=psb[b][:], func=AF.Ln, bias=cst[:, 0:1], scale=1.0 / 128), reads=[t_ps[b], t_const, te], writes=[te])

                    def e2b():
                        S.op("act", lambda h: h.activation(out=o[1][:], in_=o[5][:], func=AF.Exp, scale=-0.5), reads=[te], writes=[te])
                        S.op("dve", lambda h: h.scalar_tensor_tensor(out=o[2][:], in0=o[4][:], scalar=subg[:, 0:1], in1=o[1][:], op0=ALU.mult, op1=ALU.mult), reads=[te, t_const], writes=[te])
                        S.op("dve", lambda h: h.tensor_scalar(out=est[e][:], in0=o[2][:], scalar1=0.8, scalar2=None, op0=ALU.mult), reads=[te], writes=[te])
                        store()
                    pend.append((i + 6, e2a))
                    pend.append((i + 12, e2b))
                else:
                    def e2():
                        S.op("act", lambda h: h.activation(out=o[1][:], in_=o[1][:], func=AF.Ln), reads=[te], writes=[te])
                        S.op("act", lambda h: h.activation(out=o[1][:], in_=o[1][:], func=AF.Exp, scale=-1.0), reads=[te], writes=[te])
                        S.op("dve", lambda h: h.tensor_tensor(out=est[e][:], in0=o[0][:], in1=o[1][:], op=ALU.mult), reads=[te], writes=[te])
                        store()
                    pend.append((i + 3, e2))

            LA = 3
            load_head(0)
            load_seg(0)
            nU = len(U)
            for i in range(nU + LA):
                if i < nU:
                    u = U[i]
                    if u["newhead"] and u["hi"] + 1 < len(heads):
                        pend.append((i + LA + 1, (lambda hn_=u["hi"] + 1: load_head(hn_))))
                    if u["newseg"]:
                        k = seg_of[(u["hi"], u["ty"])]
                        first_use = (u["ty"] == 0 and u["Q"] == 0) or (u["ty"] == 1 and u["Q"] == 1) or (u["ty"] == 2)
                        if first_use and k + 1 < len(segs):
                            load_seg(k + 1)
                    stageA(i, u)
                    stageB(i, u)
                if i >= LA:
                    uc = U[i - LA]
                    stageC(i - LA, uc)
                    if uc["lastQ"]:
                        epilogue(i - LA, uc)
                while pend and pend[0][0] <= i:
                    pend.pop(0)[1]()
                pend.sort(key=lambda x: x[0])
            while pend:
                pend.pop(0)[1]()
            while cast_list:
                cast_list.pop(0)()
            S.barrier()

        if "stop2" in dbg:
            S.emit()
            return nc

        def chain(li):
            with ExitStack() as ph:
                WS = WStream(ph)
                hT = sb(ph, "c_hT", [128, NCH, T], F32)
                hn = sb(ph, "c_hn", [128, NCH, T], BF16)
                oin = hn
                act = sb(ph, "c_act", [128, 44, T], BF16)
                sq = [sb(ph, "c_sq%d" % i, [128, T], BF16) for i in range(4)]
                rs = sb(ph, "c_rs", [128, T], F32)
                rstd = sb(ph, "c_rstd", [128, T], F32)
                tm = [sb(ph, "c_tm%d" % i, [128, T], F32) for i in range(4)]
                t_tm = [Tl() for _ in range(4)]
                ptm = sb(ph, "c_ptm", [128, 4, 256], F32)
                pT = sb(ph, "c_pT", [128, 2, T], BF16)
                wproj = sb(ph, "c_wproj", [128, 2, D], BF16)
                t_hT, t_hn, t_oin, t_act, t_rs, t_ptm, t_pT, t_wp = [Tl() for _ in range(8)]
                t_oin = t_hn
                t_sq = [Tl(), Tl(), Tl(), Tl()]
                ostg = sb(ph, "c_ostg", [128, 4, 512], F32)
                t_ostg = Tl()
                kt = 0
                S.dma("sp", lambda h: h.dma_start(out=wproj[:], in_=wb["proj%d" % li].rearrange("(c p) n -> p c n", p=128)), writes=[t_wp])
                wmix = wb["ab_out"] if li == 0 else wb["c_out"]
                for tt in range(NTT):
                    t0 = tt * T
                    sidx = max(i for i in range(len(SEQS)) if seq_off[i] <= t0)
                    S.dma("sp", lambda h, t0=t0: h.dma_start(out=hT[:], in_=hscr.rearrange("(c p) t -> p c t", p=128)[:, :, t0:t0 + T]), writes=[t_hT])
                    S.dma("sp", lambda h, t0=t0: h.dma_start(out=oin[:], in_=oT.rearrange("(c p) t -> p c t", p=128)[:, :, t0:t0 + T]), writes=[t_oin])
                    S.dma("sp", lambda h, t0=t0: h.dma_start(out=ptm[:], in_=pp[li, t0:t0 + T, :].rearrange("(tb p) d -> p tb d", p=128)), writes=[t_ptm])
                    for nb in range(4):
                        wv, t_w = WS.load([(wmix[:, nb * 512:(nb + 1) * 512], NCH, 512)])
                        for j in range(4):
                            b = bank()
                            cc = nb * 4 + j
                            fns = [lambda h, c=c, j=j, b=b, wv=wv: h.matmul(psb[b][:], wv[:, c, j * 128:(j + 1) * 128], oin[:, c, :], start=(c == 0), stop=(c == NCH - 1)) for c in range(NCH)]
                            S.grp("pe", fns, reads=[t_w, t_oin], writes=[t_ps[b]])
                            S.op("dve", lambda h, cc=cc, b=b: h.tensor_tensor(out=hT[:, cc, :], in0=hT[:, cc, :], in1=psb[b][:], op=ALU.add), reads=[t_ps[b], t_hT], writes=[t_hT])
                    rmsnorm_pre(hT, t_hT, hn, t_hn, 2 + li)
                    wfi = wb["ffn_in%d" % li]
                    S.epoch(t_act)
                    epis = []
                    for u in range(22):
                        wv, t_w = WS.load([(wfi[:, u * 256:(u + 1) * 256], NCH, 256), (wfi[:, DFF + u * 256:DFF + (u + 1) * 256], NCH, 256)])
                        for j in range(2):
                            bg = bank()
                            bu = bank()
                            fns = [lambda h, c=c, j=j, b=bg, wv=wv: h.matmul(psb[b][:], wv[:, c, j * 128:(j + 1) * 128], hn[:, c, :], start=(c == 0), stop=(c == NCH - 1)) for c in range(NCH)]
                            S.grp("pe", fns, reads=[t_w, t_hn], writes=[t_ps[bg]])
                            fns = [lambda h, c=c, j=j, b=bu, wv=wv: h.matmul(psb[b][:], wv[:, c, 256 + j * 128:256 + (j + 1) * 128], hn[:, c, :], start=(c == 0), stop=(c == NCH - 1)) for c in range(NCH)]
                            S.grp("pe", fns, reads=[t_w, t_hn], writes=[t_ps[bu]])
                            it = kt % 4
                            kt += 1

                            def epi(bg=bg, bu=bu, it=it, fc=u * 2 + j):
                                S.op("dve", lambda h: h.tensor_tensor(out=tm[it][:], in0=psb[bg][:], in1=rstd[:], op=ALU.mult), reads=[t_ps[bg], t_rs], writes=[t_tm[it]])
                                S.op("act", lambda h: h.activation(out=tm[it][:], in_=tm[it][:], func=AF.Silu), reads=[t_tm[it]], writes=[t_tm[it]])
                                S.op("pool", lambda h: h.tensor_tensor(out=tm[it][:], in0=tm[it][:], in1=rstd[:], op=ALU.mult), reads=[t_tm[it], t_rs], writes=[t_tm[it]])
                                S.op("dve", lambda h: h.tensor_tensor(out=act[:, fc, :], in0=tm[it][:], in1=psb[bu][:], op=ALU.mult),
                                     reads=[t_tm[it], t_ps[bu]], writes=[t_act], add=True)
                            if u == 0:
                                epis.append(epi)
                                if j == 1:
                                    rmsnorm_stats(hT, t_hT, sq, t_sq, rs, rstd, t_rs)
                                    for e_ in epis:
                                        e_()
                            else:
                                epi()
                    wfo = wb["ffn_out%d" % li]
                    for nb in range(8):
                        b2 = [bank(), bank()]
                        for kh in range(2):
                            wv, t_w = WS.load([(wfo[kh * 22 * 128:(kh + 1) * 22 * 128, nb * 256:(nb + 1) * 256], 22, 256)])
                            for j in range(2):
                                b = b2[j]
                                fns = [lambda h, c=c, j=j, b=b, wv=wv, kh=kh: h.matmul(psb[b][:], wv[:, c, j * 128:(j + 1) * 128], act[:, kh * 22 + c, :], start=(kh == 0 and c == 0), stop=(kh == 1 and c == 21)) for c in range(22)]
                                S.grp("pe", fns, reads=[t_w, t_act], writes=[t_ps[b]])
                        for j in range(2):
                            cc = nb * 2 + j
                            S.op("dve", lambda h, cc=cc, b=b2[j]: h.tensor_tensor(out=hT[:, cc, :], in0=hT[:, cc, :], in1=psb[b][:], op=ALU.add), reads=[t_ps[b2[j]], t_hT], writes=[t_hT])
                    rmsnorm_pre(hT, t_hT, hn, t_hn, 4 + li)
                    S.epoch(t_pT)
                    transpose_in(ptm, t_ptm, pT, t_pT, 2, ident)
                    wg = wb["gate%d" % li]
                    epis = []
                    for nb in range(4):
                        wv, t_w = WS.load([(wg[:, nb * 512:(nb + 1) * 512], NCH, 512)])
                        for j in range(4):
                            cc = nb * 4 + j
                            bg = bank()
                            bp = bank()
                            fns = [lambda h, c=c, j=j, b=bg, wv=wv: h.matmul(psb[b][:], wv[:, c, j * 128:(j + 1) * 128], hn[:, c, :], start=(c == 0), stop=(c == NCH - 1)) for c in range(NCH)]
                            S.grp("pe", fns, reads=[t_w, t_hn], writes=[t_ps[bg]])
                            fns = [lambda h, c=c, cc=cc, b=bp: h.matmul(psb[b][:], wproj[:, c, cc * 128:(cc + 1) * 128], pT[:, c, :], start=(c == 0), stop=(c == 1)) for c in range(2)]
                            S.grp("pe", fns, reads=[t_wp, t_pT], writes=[t_ps[bp]])
                            it = kt % 4
                            kt += 1

                            def epi(bg=bg, bp=bp, it=it, cc=cc):
                                S.op("dve", lambda h: h.tensor_tensor(out=tm[it][:], in0=psb[bg][:], in1=rstd[:], op=ALU.mult), reads=[t_ps[bg], t_rs], writes=[t_tm[it]])
                                S.op("act", lambda h: h.activation(out=tm[it][:], in_=tm[it][:], func=AF.Sigmoid), reads=[t_tm[it]], writes=[t_tm[it]])
                                S.op("dve", lambda h: h.tensor_tensor(out=tm[it][:], in0=tm[it][:], in1=psb[bp][:], op=ALU.mult), reads=[t_tm[it], t_ps[bp]], writes=[t_tm[it]])
                                S.op("dve", lambda h: h.tensor_tensor(out=hT[:, cc, :], in0=hT[:, cc, :], in1=tm[it][:], op=ALU.add), reads=[t_tm[it], t_hT], writes=[t_hT])
                            if nb == 0 and j < 2:
                                epis.append(epi)
                                if j == 1:
                                    rmsnorm_stats(hT, t_hT, sq, t_sq, rs, rstd, t_rs)
                                    for e_ in epis:
                                        e_()
                            else:
                                epi()
                    if li == 0:
                        S.dma("pool", lambda h, t0=t0: h.dma_start(out=hscr.rearrange("(c p) t -> p c t", p=128)[:, :, t0:t0 + T], in_=hT[:]), reads=[t_hT])
                        rmsnorm_pre(hT, t_hT, hn, t_hn, 1)
                        zc0 = zoff[sidx] + 1 + (t0 - seq_off[sidx])
                        epis = []
                        for nb in range(12):
                            wv, t_w = WS.load([(wb["c_in"][:, nb * 512:(nb + 1) * 512], NCH, 512)])
                            for j in range(4):
                                b = bank()
                                fns = [lambda h, c=c, j=j, b=b, wv=wv: h.matmul(psb[b][:], wv[:, c, j * 128:(j + 1) * 128], hn[:, c, :], start=(c == 0), stop=(c == NCH - 1)) for c in range(NCH)]
                                S.grp("pe", fns, reads=[t_w, t_hn], writes=[t_ps[b]])
                                it = kt % 4
                                kt += 1
                                r0 = (nb * 4 + j) * 128

                                def epi(b=b, it=it, r0=r0, zc0=zc0):
                                    S.op("dve", lambda h: h.tensor_tensor(out=tm[it][:], in0=psb[b][:], in1=rstd[:], op=ALU.mult), reads=[t_ps[b], t_rs], writes=[t_tm[it]])
                                    S.dma("pool", lambda h: h.dma_start(out=zscr[r0:r0 + 128, zc0:zc0 + T], in_=tm[it][:]), reads=[t_tm[it]])
                                if nb == 0:
                                    epis.append(epi)
                                    if j == 3:
                                        rmsnorm_stats(hT, t_hT, sq, t_sq, rs, rstd, t_rs)
                                        for e_ in epis:
                                            e_()
                                else:
                                    epi()
                    else:
                        rmsnorm(hT, t_hT, None, None, 6, sq, t_sq, rs, rstd, t_rs)
                        for dq in range(4):
                            S.epoch(t_ostg)
                            for j in range(4):
                                cc = dq * 4 + j
                                it = kt % 4
                                kt += 1
                                S.op("dve", lambda h, it=it, cc=cc: h.scalar_tensor_tensor(out=tm[it][:], in0=hT[:, cc, :], scalar=gains[:, 6, cc:cc + 1], in1=rstd[:], op0=ALU.mult, op1=ALU.mult),
                                     reads=[t_hT, t_rs, t_const], writes=[t_tm[it]])
                                b = bank()
                                fns = [lambda h, tb=tb, b=b, it=it: h.transpose(psb[b][:, tb * 128:(tb + 1) * 128], tm[it][:, tb * 128:(tb + 1) * 128], ident[:]) for tb in range(4)]
                                S.grp("pe", fns, reads=[t_tm[it], t_const], writes=[t_ps[b]])
                                copy_op(evac_eng(), ostg[:, :, j * 128:(j + 1) * 128], psb[b][:].rearrange("p (tb n) -> p tb n", n=128), [t_ps[b]], [t_ostg], add=True)
                            S.dma("pool", lambda h, dq=dq, t0=t0: h.dma_start(out=y_out[t0:t0 + T, dq * 512:(dq + 1) * 512].rearrange("(tb p) n -> p tb n", p=128), in_=ostg[:]), reads=[t_ostg])
                S.barrier()

        chain(0)
        if "stop3" in dbg:
            S.emit()
            return nc

        def sin_reduce(ph_tiles, arg, out, npart, t_arg, t_out, split=None):
            u, ni, nf_, neg = ph_tiles
            P = slice(0, npart)
            S.op("dve", lambda h: h.tensor_scalar(out=u[P, :], in0=arg[P, :], scalar1=1.0 / (2 * math.pi), scalar2=8.5, op0=ALU.mult, op1=ALU.add), reads=[t_arg], writes=[t_arg])
            S.op("dve", lambda h: h.tensor_copy(out=ni[P, :], in_=u[P, :]), reads=[t_arg], writes=[t_arg])
            S.op("dve", lambda h: h.tensor_copy(out=nf_[P, :], in_=ni[P, :]), reads=[t_arg], writes=[t_arg])
            S.op("dve", lambda h: h.tensor_tensor(out=u[P, :], in0=u[P, :], in1=nf_[P, :], op=ALU.subtract), reads=[t_arg], writes=[t_arg])
            S.op("dve", lambda h: h.tensor_scalar(out=neg[P, :], in0=u[P, :], scalar1=0.0, scalar2=None, op0=ALU.is_lt), reads=[t_arg], writes=[t_arg])
            S.op("dve", lambda h: h.tensor_tensor(out=u[P, :], in0=u[P, :], in1=neg[P, :], op=ALU.add), reads=[t_arg], writes=[t_arg])
            if split is None:
                S.op("act", lambda h: h.activation(out=out, in_=u[P, :], func=AF.Sin, bias=cst[P, 1:2], scale=6.28318), reads=[t_arg, t_const], writes=[t_out], add=True)
            else:
                dst_, tq_, Lh_ = split
                S.op("act", lambda h: h.activation(out=dst_[:, tq_ * 256:(tq_ + 1) * 256], in_=u[P, 0:512:2], func=AF.Sin, bias=cst[P, 1:2], scale=6.28318), reads=[t_arg, t_const], writes=[t_out], add=True)
                S.op("act", lambda h: h.activation(out=dst_[:, Lh_ + tq_ * 256:Lh_ + (tq_ + 1) * 256], in_=u[P, 1:512:2], func=AF.Sin, bias=cst[P, 1:2], scale=6.28318), reads=[t_arg, t_const], writes=[t_out], add=True)

        with ExitStack() as ph0:
            cw = sb(ph0, "cw", [128, 48, 4], F32)
            skp = sb(ph0, "skp", [128, NCH], F32)
            zcol = sb(ph0, "zcol", [128, 48, 1], F32)
            altc = sb(ph0, "altc", [128, 2], BF16)
            altr = sb(ph0, "altr", [1, 512], BF16)
            altf = sb(ph0, "altf", [128, 512], F32)
            t_h0 = Tl()
            for k in range(3):
                for c3 in range(3):
                    S.dma("sp", lambda h, k=k, c3=c3: h.dma_start(out=cw[:, c3 * 16:(c3 + 1) * 16, k], in_=c_conv_w[0, k, c3 * 2048:(c3 + 1) * 2048].rearrange("(c p) -> p c", p=128), allow_slow_non_contiguous=True), writes=[t_h0], add=True)
            for c3 in range(3):
                S.dma("sp", lambda h, c3=c3: h.dma_start(out=cw[:, c3 * 16:(c3 + 1) * 16, 3], in_=c_conv_b[0, c3 * 2048:(c3 + 1) * 2048].rearrange("(c p) -> p c", p=128), allow_slow_non_contiguous=True), writes=[t_h0], add=True)
            S.dma("sp", lambda h: h.dma_start(out=skp[:], in_=c_skip[0].rearrange("(c p) -> p c", p=128), allow_slow_non_contiguous=True), writes=[t_h0], add=True)
            S.op("pool", lambda h: h.memset(zcol[:], 0.0), writes=[t_h0], add=True)
            S.dma("sp", lambda h: h.dma_start(out=altf[:], in_=alt_d[:, :]), writes=[t_h0], add=True)
            S.barrier()
            S.op("dve", lambda h: h.tensor_copy(out=altc[:, 0:1], in_=altf[:, 0:1]), reads=[t_h0], writes=[t_h0], add=True)
            S.op("dve", lambda h: h.tensor_copy(out=altr[:], in_=altf[0:1, :]), reads=[t_h0], writes=[t_h0], add=True)
            S.barrier()
            for i, L in enumerate(SEQS):
                for col in (zoff[i], zoff[i] + L + 1):
                    for c3 in range(3):
                        S.dma("pool", lambda h, col=col, c3=c3: h.dma_start(out=zscr[c3 * 2048:(c3 + 1) * 2048, col:col + 1].rearrange("(c p) t -> p c t", p=128), in_=zcol[:, 0:16, :], allow_slow_non_contiguous=True), reads=[t_h0])
            S.barrier()

            def hyena_seq(si_, L):
                s0 = seq_off[si_]
                nfb = L // 128
                nh = nfb // 2
                Lh = L // 2
                ntq = L // 512
                N = 2 * L
                Hs = Hscr[si_]
                CE, SE, CO, SO, COT, SOT = [dftm[L][k] for k in range(6)]
                def load_fwd(cfv, sfv, tl, fb):
                    for mi, (dst, mat) in enumerate(((cfv[:, 0:nh, :], CE), (cfv[:, nh:nfb, :], CO), (sfv[:, 0:nh, :], SE), (sfv[:, nh:nfb, :], SO))):
                        for k8 in range(max(1, nh // 8)):
                            w8 = min(8, nh)
                            S.dma("sp", lambda h, dst=dst, mat=mat, k8=k8, w8=w8: h.dma_start(out=dst[:, k8 * w8:(k8 + 1) * w8, :], in_=mat[k8 * w8 * 128:(k8 + 1) * w8 * 128, fb * 128:(fb + 1) * 128].rearrange("(tb p) f -> p tb f", p=128)),
                                  writes=[tl], add=not (mi == 0 and k8 == 0))

                with ExitStack() as ph:
                    hd2 = sb(ph, "hd2", [64, L], F32)
                    w3 = sb(ph, "fw3", [64, 2 * D], F32)
                    negt = sb(ph, "negt", [128, nfb], F32)
                    wn = sb(ph, "wn", [128, nh], F32)
                    delt = sb(ph, "delt", [128, D], F32)
                    t_f, t_arg, t_hd1, t_hd2, t_hs = Tl(), Tl(), Tl(), Tl(), Tl()
                    S.dma("sp", lambda h: h.dma_start(out=w3[:], in_=f_w3[0]), writes=[t_f], add=True)
                    S.dma("sp", lambda h: h.dma_start(out=negt[:], in_=negt_d[L][:, :]), writes=[t_f], add=True)
                    S.dma("sp", lambda h: h.dma_start(out=wn[:], in_=wn_d[L][:, :]), writes=[t_f], add=True)
                    S.dma("sp", lambda h: h.dma_start(out=delt[:], in_=delta_d[:, :]), writes=[t_f], add=True)
                    with ExitStack() as ph1:
                        featT = sb(ph1, "featT", [33, L], F32)
                        w1 = sb(ph1, "fw1", [33, 64], F32)
                        w2 = sb(ph1, "fw2", [64, 64], F32)
                        fb_ = sb(ph1, "fbias", [64, 4], F32)
                        hd1 = sb(ph1, "hd1", [64, L], F32)
                        argt = sb(ph1, "argt", [64, 512], F32)
                        srt = (sb(ph1, "sr_u", [64, 512], F32), sb(ph1, "sr_i", [64, 512], I32), sb(ph1, "sr_f", [64, 512], F32), sb(ph1, "sr_n", [64, 512], F32))
                        S.dma("sp", lambda h: h.dma_start(out=featT[:], in_=feat_d[L][:, :]), writes=[t_f], add=True)
                        S.dma("sp", lambda h: h.dma_start(out=w1[:], in_=f_w1[0]), writes=[t_f], add=True)
                        S.dma("sp", lambda h: h.dma_start(out=w2[:], in_=f_w2[0]), writes=[t_f], add=True)
                        S.dma("sp", lambda h: h.dma_start(out=fb_[:, 0:1], in_=f_b1[0].rearrange("(p o) -> p o", o=1), allow_slow_non_contiguous=True), writes=[t_f], add=True)
                        S.dma("sp", lambda h: h.dma_start(out=fb_[:, 1:2], in_=f_b2[0].rearrange("(p o) -> p o", o=1), allow_slow_non_contiguous=True), writes=[t_f], add=True)
                        S.dma("sp", lambda h: h.dma_start(out=fb_[:, 2:3], in_=f_freq[0, 0].rearrange("(p o) -> p o", o=1), allow_slow_non_contiguous=True), writes=[t_f], add=True)
                        S.dma("sp", lambda h: h.dma_start(out=fb_[:, 3:4], in_=f_freq[0, 1].rearrange("(p o) -> p o", o=1), allow_slow_non_contiguous=True), writes=[t_f], add=True)
                        for (src_, wgt, kdim, bc, fc, dst, t_src, t_dst) in ((featT, w1, 33, 0, 2, hd1, t_f, t_hd1), (hd1, w2, 64, 1, 3, hd2, t_hd1, t_hd2)):
                            for tq in range(ntq):
                                b = bank()
                                S.op("pe", lambda h, b=b, src_=src_, wgt=wgt, kdim=kdim, tq=tq: h.matmul(psb[b][0:64, :], wgt[0:kdim, :], src_[0:kdim, tq * 512:(tq + 1) * 512], start=True, stop=True),
                                     reads=[t_f, t_src], writes=[t_ps[b]])
                                S.op("dve", lambda h, b=b, bc=bc, fc=fc: h.tensor_scalar(out=argt[:], in0=psb[b][0:64, :], scalar1=fb_[:, bc:bc + 1], scalar2=fb_[:, fc:fc + 1], op0=ALU.add, op1=ALU.mult),
                                     reads=[t_ps[b], t_f], writes=[t_arg])
                                sin_reduce(srt, argt, dst[:, tq * 512:(tq + 1) * 512], 64, t_arg, t_dst, split=(None if dst is hd1 else (dst, tq, Lh)))
                        S.barrier()
                    hs = sb(ph, "hs", [128, nfb, 512], BF16)
                    hdd = sb(ph, "hdd", [128, nfb, 512], BF16)
                    dec = [sb(ph, "dec%d" % i, [128, 512], F32) for i in range(2)]
                    hf = [sb(ph, "hf%d" % i, [128, 512], F32) for i in range(2)]
                    hb = [sb(ph, "hb%d" % i, [128, 512], F32) for i in range(2)]
                    cf = [sb(ph, "cf%d" % i, [128, nfb, 128], BF16) for i in range(2)]
                    sf = [sb(ph, "sf%d" % i, [128, nfb, 128], BF16) for i in range(2)]
                    hst = [sb(ph, "hst%d" % i, [128, 4, 512], F32) for i in range(2)]
                    hto = [sb(ph, "hto%d" % i, [128, 2, 512], F32) for i in range(2)]
                    t_dec = [Tl(), Tl()]
                    t_cf = [Tl(), Tl()]
                    t_hst = [Tl(), Tl()]
                    t_hto = [Tl(), Tl()]
                    kd = 0
                    kc = 0
                    ks = 0

                    def _cb_a(cb):
                        nonlocal kd, kc, ks
                        S.epoch(t_hs)
                        for rb in range(nfb):
                            i = kd % 2
                            kd += 1
                            bF = bank()
                            bB = bank()
                            S.op("pe", lambda h, b=bF, rb=rb: h.matmul(psb[b][:], hd2[:, rb * 128:(rb + 1) * 128], w3[:, cb * 512:(cb + 1) * 512], start=True, stop=True), reads=[t_hd2, t_f], writes=[t_ps[bF]])
                            S.op("pe", lambda h, b=bB, rb=rb: h.matmul(psb[b][:], hd2[:, rb * 128:(rb + 1) * 128], w3[:, D + cb * 512:D + (cb + 1) * 512], start=True, stop=True), reads=[t_hd2, t_f], writes=[t_ps[bB]])
                            S.op("act", lambda h, i=i, rb=rb: h.activation(out=dec[i][:], in_=delt[:, cb * 512:(cb + 1) * 512], func=AF.Exp, scale=negt[:, rb:rb + 1]), reads=[t_f], writes=[t_dec[i]])
                            S.op("dve", lambda h, i=i, b=bF: h.tensor_tensor(out=hf[i][:], in0=psb[b][:], in1=dec[i][:], op=ALU.mult), reads=[t_ps[bF], t_dec[i]], writes=[t_dec[i]])
                            S.op("dve", lambda h, i=i, b=bB: h.tensor_tensor(out=hb[i][:], in0=psb[b][:], in1=dec[i][:], op=ALU.mult), reads=[t_ps[bB], t_dec[i]], writes=[t_dec[i]])
                            S.op("dve", lambda h, i=i, rb=rb: h.tensor_tensor(out=hdd[:, rb, :], in0=hf[i][:], in1=hb[i][:], op=ALU.subtract), reads=[t_dec[i]], writes=[t_hs], add=True)
                            if rb == 0:
                                S.op("dve", lambda h, i=i: h.tensor_scalar(out=hb[i][:], in0=hb[i][:], scalar1=cst[:, 3:4], scalar2=None, op0=ALU.mult), reads=[t_dec[i], t_const], writes=[t_dec[i]])
                            S.op("dve", lambda h, i=i, rb=rb: h.tensor_tensor(out=hs[:, rb, :], in0=hf[i][:], in1=hb[i][:], op=ALU.add), reads=[t_dec[i]], writes=[t_hs], add=True)
                        for fb in range(nh + 1):
                            half = fb == nh
                            if not half:
                                i = kc % 2
                                kc += 1
                                load_fwd(cf[i], sf[i], t_cf[i], fb)
                                bks = [bank() for _ in range(4)]
                                for q, (mt, xs_, off) in enumerate(((cf[i], hs, 0), (cf[i], hs, nh), (sf[i], hdd, 0), (sf[i], hdd, nh))):
                                    fns = [lambda h, tb=tb, b=bks[q], mt=mt, xs_=xs_, off=off: h.matmul(psb[b][:], mt[:, off + tb, :], xs_[:, off + tb, :], start=(tb == 0), stop=(tb == nh - 1)) for tb in range(nh)]
                                    S.grp("pe", fns, reads=[t_cf[i], t_hs], writes=[t_ps[bks[q]]])
                                j = ks % 2
                                ks += 1
                                S.op("act", lambda h, j=j, b=bks[1], fb=fb: h.activation(out=hto[j][:, 0, :], in_=psb[b][:], func=AF.Copy, scale=wn[:, fb:fb + 1]), reads=[t_ps[bks[1]], t_f], writes=[t_hto[j]])
                                S.op("act", lambda h, j=j, b=bks[3], fb=fb: h.activation(out=hto[j][:, 1, :], in_=psb[b][:], func=AF.Copy, scale=wn[:, fb:fb + 1]), reads=[t_ps[bks[3]], t_f], writes=[t_hto[j]], add=True)
                                S.epoch(t_hst[j])
                                for slot, bsrc, osl, op1 in ((0, bks[0], 0, ALU.add), (2, bks[0], 0, ALU.subtract), (1, bks[2], 1, ALU.add), (3, bks[2], 1, ALU.subtract)):
                                    S.op("dve", lambda h, j=j, slot=slot, bsrc=bsrc, osl=osl, op1=op1, fb=fb: h.scalar_tensor_tensor(out=hst[j][:, slot, :], in0=psb[bsrc][:], scalar=wn[:, fb:fb + 1], in1=hto[j][:, osl, :], op0=ALU.mult, op1=op1),
                                         reads=[t_ps[bsrc], t_hto[j], t_f], writes=[t_hst[j]], add=True)
                                S.dma("pool", lambda h, j=j, fb=fb: h.dma_start(out=Hs[:, fb * 128:(fb + 1) * 128, cb * 512:(cb + 1) * 512].rearrange("a p n -> p a n"), in_=hst[j][:]), reads=[t_hst[j]])
                            else:
                                bP = bank()
                                bQ = bank()
                                fns = [lambda h, tb=tb, b=bP: h.matmul(psb[b][0:1, :], altc[:, 0:1], hs[:, tb, :], start=(tb == 0), stop=(tb == nh - 1)) for tb in range(nh)]
                                S.grp("pe", fns, reads=[t_h0, t_hs], writes=[t_ps[bP]])
                                fns = [lambda h, tb=tb, b=bQ: h.matmul(psb[b][0:1, :], altc[:, 0:1], hdd[:, nh + tb, :], start=(tb == 0), stop=(tb == nh - 1)) for tb in range(nh)]
                                S.grp("pe", fns, reads=[t_h0, t_hs], writes=[t_ps[bQ]])
                                j = ks % 2
                                ks += 1
                                S.op("act", lambda h, j=j, b=bP: h.activation(out=hst[j][0:1, 0, :], in_=psb[b][0:1, :], func=AF.Copy, scale=2.0 / N), reads=[t_ps[bP]], writes=[t_hst[j]])
                                S.op("act", lambda h, j=j, b=bQ: h.activation(out=hst[j][0:1, 1, :], in_=psb[b][0:1, :], func=AF.Copy, scale=2.0 / N), reads=[t_ps[bQ]], writes=[t_hst[j]], add=True)
                                S.dma("pool", lambda h, j=j: h.dma_start(out=Hs[0:2, Lh:Lh + 1, cb * 512:(cb + 1) * 512].rearrange("a p n -> p a n"), in_=hst[j][0:1, 0:2, :]), reads=[t_hst[j]])
                    for cb in range(4):
                        _cb_a(cb)
                    S.barrier()
                if "stop4a" in dbg:
                    return
                with ExitStack() as ph:
                    vTM = sb(ph, "vTM", [128, nfb, 512], BF16)
                    G = [sb(ph, "G%d" % i, [128, nh, 512], BF16) for i in range(4)]
                    YLh = sb(ph, "YLh", [1, 2, 512], BF16)
                    dft = [sb(ph, "dft%d" % i, [128, 2, 4096], BF16) for i in range(2)]
                    t_dft = [Tl(), Tl()]
                    zh = [sb(ph, "zh%d" % i, [128, 514], F32) for i in range(4)]
                    t_zh = [Tl() for _ in range(4)]
                    cv = [sb(ph, "cv%d" % i, [128, 512], F32) for i in range(4)]
                    t_cv = [Tl() for _ in range(4)]
                    vb = [sb(ph, "vb%d" % i, [128, 512], BF16) for i in range(2)]
                    t_vb = [Tl(), Tl()]
                    abx = [sb(ph, "abx%d" % i, [128, 512], F32) for i in range(2)]
                    t_abx = [Tl(), Tl()]
                    hPs = [sb(ph, "hP%d" % i, [128, 4, 512], F32) for i in range(2)]
                    t_hPs = [Tl(), Tl()]
                    khp = 0
                    wk = [sb(ph, "wk%d" % i, [128, 512], F32) for i in range(6)]
                    t_wkd, t_wkp, t_wk4, t_wk5 = Tl(), Tl(), Tl(), Tl()
                    yo = [sb(ph, "yo%d" % i, [128, 512], BF16) for i in range(2)]
                    t_yo = [Tl(), Tl()]
                    t_vTM, t_Y = Tl(), Tl()
                    kz = 0
                    kv = 0
                    kdf = 0
                    kyo = 0

                    def conv(zi, chunk, out, t_out_list):
                        S.op("act", lambda h: h.activation(out=out, in_=zh[zi][:, 0:512], func=AF.Identity, bias=cw[:, chunk, 3:4], scale=cw[:, chunk, 0:1]), reads=[t_zh[zi], t_h0], writes=t_out_list)
                        S.op("dve", lambda h: h.scalar_tensor_tensor(out=out, in0=zh[zi][:, 1:513], scalar=cw[:, chunk, 1:2], in1=out, op0=ALU.mult, op1=ALU.add), reads=[t_zh[zi], t_h0] + t_out_list, writes=t_out_list)
                        S.op("dve", lambda h: h.scalar_tensor_tensor(out=out, in0=zh[zi][:, 2:514], scalar=cw[:, chunk, 2:3], in1=out, op0=ALU.mult, op1=ALU.add), reads=[t_zh[zi], t_h0] + t_out_list, writes=t_out_list)

                    def tt(eng, out, a, b_, op, reads, writes, add=False):
                        S.op(eng, lambda h: h.tensor_tensor(out=out, in0=a, in1=b_, op=op), reads=reads, writes=writes, add=add)

                    def _cb_b(cb):
                        nonlocal kz, kv, kdf, kyo, khp
                        pendG = [None]
                        t_vf = Tl()
                        S.epoch(t_vTM)
                        for tq in range(ntq):
                            zc = zoff[si_] + tq * 512
                            for cc in range(4):
                                ch = cb * 4 + cc
                                z1 = kz % 4
                                z2 = (kz + 1) % 4
                                kz += 2
                                S.dma("sp", lambda h, z1=z1, ch=ch, zc=zc: h.dma_start(out=zh[z1][:], in_=zscr[(16 + ch) * 128:(17 + ch) * 128, zc:zc + 514]), writes=[t_zh[z1]])
                                S.dma("sp", lambda h, z2=z2, ch=ch, zc=zc: h.dma_start(out=zh[z2][:], in_=zscr[(32 + ch) * 128:(33 + ch) * 128, zc:zc + 514]), writes=[t_zh[z2]])
                                c1 = kv % 4
                                c2 = (kv + 1) % 4
                                kv += 2
                                conv(z1, 16 + ch, cv[c1][:], [t_cv[c1]])
                                conv(z2, 32 + ch, cv[c2][:], [t_cv[c2]])
                                S.op("pool", lambda h, c1=c1, c2=c2: h.tensor_tensor(out=cv[c1][:], in0=cv[c1][:], in1=cv[c2][:], op=ALU.mult), reads=[t_cv[c1], t_cv[c2]], writes=[t_cv[c1]])
                                S.dma("pool", lambda h, c1=c1, ch=ch, tq=tq: h.dma_start(out=vfm[ch * 128:(ch + 1) * 128, s0 + tq * 512:s0 + (tq + 1) * 512], in_=cv[c1][:]), reads=[t_cv[c1]], writes=[t_vf], add=True)
                                iv = (kv // 2) % 2
                                S.op("act", lambda h, iv=iv, c1=c1: h.copy(out=vb[iv][:, 0:256], in_=cv[c1][:, 0:512:2]), reads=[t_cv[c1]], writes=[t_vb[iv]])
                                S.op("act", lambda h, iv=iv, c1=c1: h.copy(out=vb[iv][:, 256:512], in_=cv[c1][:, 1:512:2]), reads=[t_cv[c1]], writes=[t_vb[iv]], add=True)
                                b = bank()
                                pbf = psb[b][:].bitcast(BF16)
                                fns = [lambda h, k=k, iv=iv, pbf=pbf: h.transpose(pbf[:, k * 128:(k + 1) * 128], vb[iv][:, k * 128:(k + 1) * 128], identb[:]) for k in range(4)]
                                S.grp("pe", fns, reads=[t_vb[iv], t_const], writes=[t_ps[b]])
                                copy_op(evac_eng(), vTM[:].rearrange("p (g b) n -> p g b n", g=2)[:, :, tq * 2:tq * 2 + 2, cc * 128:(cc + 1) * 128],
                                        pbf[:, 0:512].rearrange("p (g b n) -> p g b n", g=2, b=2), [t_ps[b]], [t_vTM], add=True)
                        if "x_nofwd" in dbg:
                            return
                        S.epoch(t_Y)
                        for fb in range(nh + 1):
                            half = fb == nh
                            hP = hPs[khp % 2]
                            t_hP = t_hPs[khp % 2]
                            khp += 1
                            if not half:
                                i = kdf % 2
                                kdf += 1
                                cfv = dft[i][:, 0, 0:nfb * 128].rearrange("p (tb f) -> p tb f", f=128)
                                sfv = dft[i][:, 1, 0:nfb * 128].rearrange("p (tb f) -> p tb f", f=128)
                                load_fwd(cfv, sfv, t_dft[i], fb)
                                S.dma("sp", lambda h, fb=fb, hP=hP: h.dma_start(out=hP[:], in_=Hs[:, fb * 128:(fb + 1) * 128, cb * 512:(cb + 1) * 512].rearrange("a p n -> p a n")), writes=[t_hP])
                                bks = [bank() for _ in range(4)]
                                for q, (mt, off) in enumerate(((cfv, 0), (cfv, nh), (sfv, 0), (sfv, nh))):
                                    fns = [lambda h, tb=tb, b=bks[q], mt=mt, off=off: h.matmul(psb[b][:], mt[:, off + tb, :], vTM[:, off + tb, :], start=(tb == 0), stop=(tb == nh - 1)) for tb in range(nh)]
                                    S.grp("pe", fns, reads=[t_dft[i], t_vTM], writes=[t_ps[bks[q]]])
                                zA, zB = zh[0][:, 0:512], zh[1][:, 0:512]
                                S.op("act", lambda h, b=bks[1], zA=zA: h.copy(out=zA, in_=psb[b][:]), reads=[t_ps[bks[1]]], writes=[t_zh[0]])
                                S.op("act", lambda h, b=bks[3], zB=zB: h.copy(out=zB, in_=psb[b][:]), reads=[t_ps[bks[3]]], writes=[t_zh[1]])
                                hs_ = khp % 2
                                Alo, Blo = cv[0][:], cv[2][:]
                                Ahi, Bhi = (cv[1][:], cv[3][:]) if hs_ == 0 else (abx[0][:], abx[1][:])
                                tAh, tBh = (t_cv[1], t_cv[3]) if hs_ == 0 else (t_abx[0], t_abx[1])
                                tt("dve", Alo, psb[bks[0]][:], zA, ALU.add, [t_ps[bks[0]], t_zh[0]], [t_cv[0]])
                                tt("dve", Ahi, psb[bks[0]][:], zA, ALU.subtract, [t_ps[bks[0]], t_zh[0]], [tAh])
                                tt("dve", Blo, psb[bks[2]][:], zB, ALU.add, [t_ps[bks[2]], t_zh[1]], [t_cv[2]])
                                tt("dve", Bhi, psb[bks[2]][:], zB, ALU.subtract, [t_ps[bks[2]], t_zh[1]], [tBh])
                                if pendG[0] is not None:
                                    pendG[0]()
                                    pendG[0] = None
                                for (eng, A_, B_, tA_, tB_, Ps, Qs, o_r, o_i, s1, s2, ts1, ts2, two) in (
                                        ("dve", Alo, Blo, t_cv[0], t_cv[2], hP[:, 0, :], hP[:, 1, :], wk[0][:], wk[1][:], zh[2][:, 0:512], zh[3][:, 0:512], t_zh[2], t_zh[3], t_wkd),
                                        ("pool", Ahi, Bhi, tAh, tBh, hP[:, 2, :], hP[:, 3, :], wk[2][:], wk[3][:], wk[4][:], wk[5][:], t_wk4, t_wk5, t_wkp)):
                                    tt(eng, s1, A_, Ps, ALU.mult, [tA_, t_hP], [ts1])
                                    tt(eng, s2, B_, Qs, ALU.mult, [tB_, t_hP], [ts2])
                                    tt(eng, o_r, s1, s2, ALU.subtract, [ts1, ts2], [two])
                                    tt(eng, s1, A_, Qs, ALU.mult, [tA_, t_hP], [ts1])
                                    tt(eng, s2, B_, Ps, ALU.mult, [tB_, t_hP], [ts2])
                                    tt(eng, o_i, s1, s2, ALU.add, [ts1, ts2], [two], add=True)
                                def gops(fb=fb):
                                    tt("dve", G[0][:, fb, :], wk[0][:], wk[2][:], ALU.add, [t_wkd, t_wkp], [t_Y], add=True)
                                    tt("dve", G[2][:, fb, :], wk[0][:], wk[2][:], ALU.subtract, [t_wkd, t_wkp], [t_Y], add=True)
                                    tt("pool", G[1][:, fb, :], wk[1][:], wk[3][:], ALU.add, [t_wkd, t_wkp], [t_Y], add=True)
                                    tt("pool", G[3][:, fb, :], wk[1][:], wk[3][:], ALU.subtract, [t_wkd, t_wkp], [t_Y], add=True)
                                pendG[0] = gops
                            else:
                                if pendG[0] is not None:
                                    pendG[0]()
                                    pendG[0] = None
                                bA = bank()
                                bB = bank()
                                S.dma("sp", lambda h, hP=hP: h.dma_start(out=hP[0:1, 0:2, :], in_=Hs[0:2, Lh:Lh + 1, cb * 512:(cb + 1) * 512].rearrange("a p n -> p a n")), writes=[t_hP])
                                fns = [lambda h, tb=tb, b=bA: h.matmul(psb[b][0:1, :], altc[:, 0:1], vTM[:, tb, :], start=(tb == 0), stop=(tb == nh - 1)) for tb in range(nh)]
                                S.grp("pe", fns, reads=[t_h0, t_vTM], writes=[t_ps[bA]])
                                fns = [lambda h, tb=tb, b=bB: h.matmul(psb[b][0:1, :], altc[:, 0:1], vTM[:, nh + tb, :], start=(tb == 0), stop=(tb == nh - 1)) for tb in range(nh)]
                                S.grp("pe", fns, reads=[t_h0, t_vTM], writes=[t_ps[bB]])
                                s_ap, s_bq, s_aq, s_bp = wk[4][0:1, :], wk[5][0:1, :], zh[2][0:1, 0:512], zh[3][0:1, 0:512]
                                tt("dve", s_ap, psb[bA][0:1, :], hP[0:1, 0, :], ALU.mult, [t_ps[bA], t_hP], [t_wk4])
                                tt("dve", s_bq, psb[bB][0:1, :], hP[0:1, 1, :], ALU.mult, [t_ps[bB], t_hP], [t_wk5])
                                tt("dve", s_aq, psb[bA][0:1, :], hP[0:1, 1, :], ALU.mult, [t_ps[bA], t_hP], [t_zh[2]])
                                tt("dve", s_bp, psb[bB][0:1, :], hP[0:1, 0, :], ALU.mult, [t_ps[bB], t_hP], [t_zh[3]])
                                tt("dve", YLh[:, 0, :], s_ap, s_bq, ALU.subtract, [t_wk4, t_wk5], [t_Y], add=True)
                                tt("dve", YLh[:, 1, :], s_aq, s_bp, ALU.add, [t_zh[2], t_zh[3]], [t_Y], add=True)
                        if "x_noinv" in dbg:
                            return
                        FG = 4
                        nfg = nh // FG
                        for sp in range(L // 1024):
                            for ccp in range(2):
                                bk = {}
                                for cc in (2 * ccp, 2 * ccp + 1):
                                    bk[(cc, 0)] = bank()
                                    bk[(cc, 1)] = bank()
                                for fg in range(nfg):
                                    i = kdf % 2
                                    kdf += 1
                                    mv = dft[i][:].rearrange("p a (b f t) -> p (a b) f t", b=2, f=FG)
                                    for mi, mat in enumerate((CE, SE, COT, SOT)):
                                        S.dma("sp", lambda h, mv=mv, mi=mi, mat=mat, fg=fg, sp=sp: h.dma_start(out=mv[:, mi, :, :], in_=mat[fg * FG * 128:(fg + 1) * FG * 128, sp * 512:(sp + 1) * 512].rearrange("(fb p) t -> p fb t", p=128)),
                                              writes=[t_dft[i]], add=(mi > 0))
                                    for cc in (2 * ccp, 2 * ccp + 1):
                                        for par in (0, 1):
                                            b = bk[(cc, par)]
                                            fns = []
                                            for f in range(FG):
                                                fb = fg * FG + f
                                                fns.append(lambda h, b=b, fb=fb, f=f, cc=cc, mv=mv, par=par, st_=(fg == 0 and f == 0): h.matmul(psb[b][:], G[2 * par][:, fb, cc * 128:(cc + 1) * 128], mv[:, 2 * par, f, :], start=st_, stop=False))
                                                fns.append(lambda h, b=b, fb=fb, f=f, cc=cc, mv=mv, par=par: h.matmul(psb[b][:], G[2 * par + 1][:, fb, cc * 128:(cc + 1) * 128], mv[:, 2 * par + 1, f, :], start=False, stop=False))
                                            if fg == nfg - 1:
                                                fns.append(lambda h, b=b, cc=cc, par=par: h.matmul(psb[b][:], YLh[0:1, par, cc * 128:(cc + 1) * 128], altr[0:1, :], start=False, stop=True))
                                            S.grp("pe", fns, reads=[t_dft[i], t_Y, t_h0], writes=[t_ps[b]])
                                for cc in (2 * ccp, 2 * ccp + 1):
                                    ch = cb * 4 + cc
                                    for hlf in range(2):
                                        tq = sp * 2 + hlf
                                        zc = zoff[si_] + tq * 512
                                        z1 = kz % 4
                                        kz += 1
                                        S.dma("sp", lambda h, z1=z1, ch=ch, zc=zc: h.dma_start(out=zh[z1][:], in_=zscr[ch * 128:(ch + 1) * 128, zc:zc + 514]), writes=[t_zh[z1]])
                                        c1 = kv % 4
                                        c2 = (kv + 1) % 4
                                        kv += 2
                                        conv(z1, ch, cv[c1][:], [t_cv[c1]])
                                        S.dma("sp", lambda h, c2=c2, ch=ch, tq=tq: h.dma_start(out=cv[c2][:], in_=vfm[ch * 128:(ch + 1) * 128, s0 + tq * 512:s0 + (tq + 1) * 512]), reads=[t_vf], writes=[t_cv[c2]])
                                        for par in (0, 1):
                                            b = bk[(cc, par)]
                                            S.op("dve", lambda h, c2=c2, ch=ch, b=b, par=par, hlf=hlf: h.scalar_tensor_tensor(out=cv[c2][:, par:512:2], in0=cv[c2][:, par:512:2], scalar=skp[:, ch:ch + 1], in1=psb[b][:, hlf * 256:(hlf + 1) * 256], op0=ALU.mult, op1=ALU.add),
                                                 reads=[t_cv[c2], t_ps[b], t_h0], writes=[t_cv[c2]])
                                        io = kyo % 2
                                        kyo += 1
                                        S.op("dve", lambda h, c1=c1, c2=c2, io=io: h.tensor_tensor(out=yo[io][:], in0=cv[c2][:], in1=cv[c1][:], op=ALU.mult), reads=[t_cv[c1], t_cv[c2]], writes=[t_yo[io]])
                                        S.dma("pool", lambda h, io=io, ch=ch, tq=tq: h.dma_start(out=oT[ch * 128:(ch + 1) * 128, s0 + tq * 512:s0 + (tq + 1) * 512], in_=yo[io][:]), reads=[t_yo[io]])
                                        if cast_late:
                                            cast_late.pop(0)()
                    for cb in range(4):
                        _cb_b(cb)
                    S.barrier()
            for si_, L in enumerate(SEQS):
                hyena_seq(si_, L)
            while cast_late:
                cast_late.pop(0)()
            S.barrier()
        if "stop4a" in dbg:
            S.emit()
            return nc
        if "stop4" in dbg:
            S.emit()
            return nc
        chain(1)
        S.emit()
    return nc


_CACHE = {}


def _consts(SEQS, inp):
    c = {}
    c["k_ident"] = np.eye(128, dtype=np.float32)
    t5t, t5f = t5_tiles(np.asarray(inp["rel_bias_table"], np.float32))
    c["k_t5t"] = t5t
    c["k_t5f"] = t5f
    c["k_nat"] = na_tiles(np.asarray(inp["na_bias"], np.float32)[0])
    pj = np.arange(128)[:, None] + np.arange(512)[None, :]
    c["k_alt"] = (1.0 - 2.0 * (pj % 2)).astype(np.float32)
    c["k_delta"] = np.ascontiguousarray(np.broadcast_to(decay_deltas()[None, :], (128, D))).astype(np.float32)
    for L in sorted(set(SEQS)):
        for k, mtx in enumerate(dft_mats(L)):
            c["k_dft%d_%d" % (L, k)] = mtx
        ft, negt = filt_consts(L)
        c["k_feat%d" % L] = ft
        c["k_negt%d" % L] = negt
        nh = L // 256
        wn = np.full((L // 2,), 2.0 / (2 * L), np.float32)
        wn[0] = 1.0 / (2 * L)
        c["k_wn%d" % L] = np.ascontiguousarray(wn.reshape(nh, 128).T)
    return c


WNAMES = ["norm_mix", "norm_ffn", "norm_ple", "final_norm", "ab_w_in", "ab_w_out", "diff_lambda", "diff_subln",
          "c_w_in", "c_conv_w", "c_conv_b", "c_filt_w1", "c_filt_b1", "c_filt_freq", "c_filt_w2", "c_filt_b2",
          "c_filt_w3", "c_skip", "c_w_out", "ffn_w_in", "ffn_w_out", "ple_w_proj", "ple_w_gate"]


def run_cores(SEQS, core_inputs, inp, dbg=()):
    key = (tuple(SEQS), tuple(dbg))
    if key not in _CACHE:
        _CACHE[key] = build_program(list(SEQS), dbg)
    nc = _CACHE[key]
    shared = {k: np.ascontiguousarray(np.asarray(inp[k], np.float32)) for k in WNAMES}
    shared.update(_consts(SEQS, inp))
    in_maps = []
    for xs, pp in core_inputs:
        m = dict(shared)
        m["xs"] = np.ascontiguousarray(xs, np.float32)
        m["pp"] = np.ascontiguousarray(pp, np.float32)
        in_maps.append(m)
    res = run_bass_kernel_spmd(nc, in_maps, core_ids=list(range(len(in_maps))))
    return res.results


def kernel(**inp):
    xp = np.asarray(inp["x_prompt"], np.float32)
    xsm = np.asarray(inp["x_sample"], np.float32)
    ppr = np.asarray(inp["p_prompt"], np.float32)
    psm = np.asarray(inp["p_sample"], np.float32)
    B, LP, _ = xp.shape
    LSm = xsm.shape[1]
    SEQS = [LP, LSm]
    core_inputs = []
    for i in range(B):
        xs = np.concatenate([xp[i], xsm[i]], axis=0)
        pp = np.concatenate([ppr[:, i], psm[:, i]], axis=1)
        core_inputs.append((xs, pp))
    res = run_cores(SEQS, core_inputs, inp)
    yp = np.stack([res[i]["y"][:LP] for i in range(B)], axis=0).astype(np.float32)
    ys = np.stack([res[i]["y"][LP:] for i in range(B)], axis=0).astype(np.float32)
    return (yp, ys)
```
